# Optimizing a Trainium2 kernel written in Bass

```python
import math
import jax, jax.numpy as jnp
from jax import lax
import numpy as np

D_MODEL = 1024
BATCH = 32
SEQ = 2048
DEPTH = 1

N_HEADS = 8
HEAD_DIM = 64
QK_DIM = 2 * N_HEADS * HEAD_DIM
V_DIM = N_HEADS * 2 * HEAD_DIM
BLOCK_Q = 128
CONV_WIDTH = D_MODEL
CONV_K = 3
D_FF = -(-8 * D_MODEL // (3 * 256)) * 256
EPS = 1e-6

IN_SIZES = [QK_DIM, QK_DIM, V_DIM, CONV_WIDTH, CONV_WIDTH, CONV_WIDTH, D_MODEL, D_MODEL]
IN_SPLITS = [int(s) for s in np.cumsum(IN_SIZES)[:-1]]
D_IN = int(sum(IN_SIZES))

kernel_name = "hybrid_conv_diffattn_gated_adaln_block"


def rmsnorm(x, g):
    xf = x.astype(jnp.float32)
    y = xf * lax.rsqrt(jnp.mean(xf * xf, axis=-1, keepdims=True) + EPS)
    return (y * g.astype(jnp.float32)).astype(x.dtype)


def causal_dwconv(u, w):
    S = u.shape[1]
    up = jnp.pad(u, ((0, 0), (CONV_K - 1, 0), (0, 0)))
    y = up[:, 0:S] * w[0]
    for j in range(1, CONV_K):
        y = y + up[:, j:j + S] * w[j]
    return y


def diff_attention(q, k, v, lam):
    S = q.shape[1]
    scale = HEAD_DIM ** -0.5
    outs = []
    for i in range(S // BLOCK_Q):
        q0 = i * BLOCK_Q
        kl = q0 + BLOCK_Q
        qb = q[:, q0:kl]
        kb = k[:, :kl]
        vb = v[:, :kl]
        s = jnp.einsum('bqhcd,bkhcd->bhcqk', qb, kb).astype(jnp.float32) * scale
        causal = (q0 + jnp.arange(BLOCK_Q))[:, None] >= jnp.arange(kl)[None, :]
        p = jax.nn.softmax(jnp.where(causal, s, -jnp.inf), axis=-1)
        a = (p[:, :, 0] - lam * p[:, :, 1]).astype(vb.dtype)
        outs.append(jnp.einsum('bhqk,bkhe->bqhe', a, vb))
    return jnp.concatenate(outs, axis=1)


def setup_inputs(seed: int = 0) -> dict:
    key = jax.random.key(seed)
    ks = jax.random.split(key, 20)
    f32 = jnp.float32
    L, D = DEPTH, D_MODEL

    def w(k, shape, fan_in, mult=1.0):
        return jax.random.normal(k, shape, f32) * (mult * fan_in ** -0.5)

    def gain(k, shape):
        return 1.0 + 0.02 * jax.random.normal(k, shape, f32)

    return {
        "x": jax.random.normal(ks[0], (BATCH, SEQ, D), f32),
        "c": jax.random.normal(ks[1], (BATCH, D), f32),
        "w_ada": w(ks[2], (L, D, 6 * D), D, 0.1),
        "b_ada": 0.02 * jax.random.normal(ks[3], (L, 6 * D), f32),
        "norm1_g": gain(ks[4], (L, D)),
        "w_in": w(ks[5], (L, D, D_IN), D),
        "conv_w": w(ks[6], (L, CONV_K, CONV_WIDTH), CONV_K),
        "q_norm_g": gain(ks[7], (L, HEAD_DIM)),
        "k_norm_g": gain(ks[8], (L, HEAD_DIM)),
        "lambda_q1": 0.1 * jax.random.normal(ks[9], (L, HEAD_DIM), f32),
        "lambda_k1": 0.1 * jax.random.normal(ks[10], (L, HEAD_DIM), f32),
        "lambda_q2": 0.1 * jax.random.normal(ks[11], (L, HEAD_DIM), f32),
        "lambda_k2": 0.1 * jax.random.normal(ks[12], (L, HEAD_DIM), f32),
        "subln_g": gain(ks[13], (L, 2 * HEAD_DIM)),
        "w_a_out": w(ks[14], (L, CONV_WIDTH, D), CONV_WIDTH),
        "w_b_out": w(ks[15], (L, V_DIM, D), V_DIM),
        "w_o": w(ks[16], (L, D, D), D),
        "norm2_g": gain(ks[17], (L, D)),
        "w_gu": w(ks[18], (L, D, 2 * D_FF), D),
        "w_down": w(ks[19], (L, D_FF, D), D_FF),
    }


def reference(x, c, w_ada, b_ada, norm1_g, w_in, conv_w, q_norm_g, k_norm_g,
              lambda_q1, lambda_k1, lambda_q2, lambda_k2, subln_g,
              w_a_out, w_b_out, w_o, norm2_g, w_gu, w_down):
    B, S, D = x.shape
    c_act = jax.nn.silu(c)
    for l in range(DEPTH):
        lambda_init = 0.8 - 0.6 * math.exp(-0.3 * l)
        mod = c_act @ w_ada[l] + b_ada[l]
        sh1, sc1, g1, sh2, sc2, g2 = [m[:, None, :] for m in jnp.split(mod, 6, axis=-1)]

        h = rmsnorm(x, norm1_g[l]) * (1.0 + sc1) + sh1
        proj = h @ w_in[l]
        q, k, v, cb, cc, cx, ga, gb = jnp.split(proj, IN_SPLITS, axis=-1)

        ya = cb * causal_dwconv(cc * cx, conv_w[l])
        ya = ya @ w_a_out[l]

        q = rmsnorm(q.reshape(B, S, N_HEADS, 2, HEAD_DIM), q_norm_g[l])
        k = rmsnorm(k.reshape(B, S, N_HEADS, 2, HEAD_DIM), k_norm_g[l])
        v = v.reshape(B, S, N_HEADS, 2 * HEAD_DIM)
        lam = (jnp.exp(jnp.sum(lambda_q1[l].astype(jnp.float32) * lambda_k1[l].astype(jnp.float32)))
               - jnp.exp(jnp.sum(lambda_q2[l].astype(jnp.float32) * lambda_k2[l].astype(jnp.float32)))
               + lambda_init)
        o = diff_attention(q, k, v, lam)
        o = rmsnorm(o, subln_g[l]) * (1.0 - lambda_init)
        yb = o.reshape(B, S, V_DIM) @ w_b_out[l]

        m = jax.nn.sigmoid(ga) * ya + jax.nn.sigmoid(gb) * yb
        x = x + g1 * (m @ w_o[l])

        h2 = rmsnorm(x, norm2_g[l]) * (1.0 + sc2) + sh2
        gu = h2 @ w_gu[l]
        fg, fu = jnp.split(gu, 2, axis=-1)
        x = x + g2 * ((jax.nn.silu(fg) * fu) @ w_down[l])
    return x
```

```python
import contextlib
import numpy as np
import concourse.bass as bass
import concourse.mybir as mybir
from concourse.bass_utils import run_bass_kernel_spmd

F32 = mybir.dt.float32
BF16 = mybir.dt.bfloat16
AF = mybir.ActivationFunctionType
ALU = mybir.AluOpType
AX = mybir.AxisListType

D = 1024
KC = 8
H = 8
DFF = 2816
FC = 22
EPS = 1e-6
LAMBDA_INIT = 0.2
N_CORES = 8
INTERLEAVE_PROJ = True
POOL_OFFLOAD = False


class T:
    __slots__ = ("ap", "w", "r", "dsem")

    def __init__(self, ap, deps=None):
        self.ap = ap
        self.w = dict(deps) if deps else {}
        self.r = {}
        self.dsem = None


class DSem:
    def __init__(self, sem):
        self.sem = sem
        self.val = 0
        self.key = "d%d" % id(self)


class Eng:
    def __init__(self, name, eng, sem, self_sync):
        self.name = name
        self.eng = eng
        self.sem = sem
        self.count = 0
        self.waited = {}
        self.self_sync = self_sync


class Sched:
    def __init__(self, nc, es):
        self.nc = nc
        self.es = es
        self.nsem = 0

        def mk(name, eng, ss):
            return Eng(name, eng, self.newsem("e_" + name), ss)

        self.pe = mk("pe", nc.tensor, False)
        self.act = mk("act", nc.scalar, True)
        self.dve = mk("dve", nc.vector, True)
        self.pool = mk("pool", nc.gpsimd, True)
        self.sp = mk("sp", nc.sync, True)

    def newsem(self, name):
        self.nsem += 1
        return self.es.enter_context(self.nc.semaphore(name))

    def dsem(self, name):
        return DSem(self.newsem(name))

    @staticmethod
    def _merge(d, src):
        for k, v in src.items():
            o = d.get(k)
            if o is None or o[1] < v[1]:
                d[k] = v

    def _wait(self, E, deps):
        for k, (sem, val) in deps.items():
            if k == E.name and not E.self_sync:
                continue
            if E.waited.get(k, 0) >= val:
                continue
            E.eng.wait_ge(sem, val)
            E.waited[k] = val

    def op(self, E, fn, reads=(), writes=(), signal=True):
        deps = {}
        for t in reads:
            self._merge(deps, t.w)
        for t in writes:
            self._merge(deps, t.w)
            self._merge(deps, t.r)
        self._wait(E, deps)
        ins = fn()
        if signal:
            E.count += 1
            ins.then_inc(E.sem, 1)
            tk = (E.sem, E.count)
        else:
            tk = (E.sem, E.count + 1)
        k = E.name
        for t in reads:
            o = t.r.get(k)
            if o is None or o[1] < tk[1]:
                t.r[k] = tk
        for t in writes:
            t.w = {k: tk}
            t.r = {}
        return ins

    def dma(self, Q, out_ap, in_ap, ds, reads=(), writes=()):
        deps = {}
        for t in reads:
            self._merge(deps, t.w)
        for t in writes:
            self._merge(deps, t.w)
            self._merge(deps, t.r)
        self._wait(Q, deps)
        ins = Q.eng.dma_start(out=out_ap, in_=in_ap)
        ds.val += 16
        ins.then_inc(ds.sem, 16)
        tk = (ds.sem, ds.val)
        k = ds.key
        for t in reads:
            o = t.r.get(k)
            if o is None or o[1] < tk[1]:
                t.r[k] = tk
        for t in writes:
            t.w = {k: tk}
            t.r = {}
        return ins

    def retire(self, tiles):
        deps = {}
        for t in tiles:
            self._merge(deps, t.w)
            self._merge(deps, t.r)
        return deps


class Region:
    def __init__(self, S_, nc, es, name, nbytes):
        self.S = S_
        self.nbytes = nbytes
        self.t = es.enter_context(nc.sbuf_tensor(name, [128, nbytes // 2], BF16))
        self.tiles = []
        self.fence = {}

    def view(self, off, shape, dt):
        n = int(np.prod(shape[1:]))
        nb = n * (4 if dt == F32 else 2)
        assert off % 4 == 0 and off + nb <= self.nbytes, (off, nb, self.nbytes)
        ap = self.t[:, off // 2:(off + nb) // 2]
        if dt == F32:
            ap = ap.bitcast(F32)
        if len(shape) == 3:
            ap = ap.rearrange("p (a b) -> p a b", b=shape[2])
        elif len(shape) == 4:
            ap = ap.rearrange("p (a b c) -> p a b c", b=shape[2], c=shape[3])
        return ap

    def tile(self, ap):
        t = T(ap, self.fence)
        self.tiles.append(t)
        return t

    def reset(self):
        f = self.S.retire(self.tiles)
        self.S._merge(f, self.fence)
        self.fence = f
        self.tiles = []


def build(NB=4, SQ=2048, debug=False):
    TB = SQ // 128
    TG = SQ // 512
    HS = max(512, SQ // 2)
    nc = bass.Bass("TRN2", target_bir_lowering=False)

    def din(n, sh):
        return nc.dram_tensor(n, sh, F32, kind="ExternalInput").ap()

    x_d = din("x", [NB, SQ, D])
    cT_d = din("cT", [128, KC, NB])
    wada_d = din("w_ada", [D, 6 * D])
    badaT_d = din("b_adaT", [128, 48])
    badar_d = din("b_ada_row", [1, 6 * D])
    n1g_d = din("n1gT", [128, KC])
    n2g_d = din("n2gT", [128, KC])
    win_d = din("w_in", [D, 8 * D])
    cw_d = din("cwT", [128, 3, KC])
    gq_d = din("gq", [128, 1])
    gk_d = din("gk", [128, 1])
    lamv_d = din("lamv", [1, 256])
    subg_d = din("subg", [128, 1])
    wa_d = din("w_a_out", [D, D])
    wb_d = din("w_b_out", [D, D])
    wo_d = din("w_o", [D, D])
    wgu_d = din("w_gu", [D, 2 * DFF])
    wdn_d = din("w_down", [DFF, D])
    out_d = nc.dram_tensor("out", [NB, SQ, D], F32, kind="ExternalOutput").ap()
    gbc_d = nc.dram_tensor("gbc", [NB, 2, 128, D], F32, kind="Internal").ap()

    kv = "(kc p) n -> p kc n"
    wada_v = wada_d.rearrange(kv, p=128)
    win_v = win_d.rearrange(kv, p=128)
    wa_v = wa_d.rearrange(kv, p=128)
    wb_v = wb_d.rearrange(kv, p=128)
    wo_v = wo_d.rearrange(kv, p=128)
    wgu_v = wgu_d.rearrange(kv, p=128)
    wdn_v = wdn_d.rearrange(kv, p=128)

    with contextlib.ExitStack() as es:
        S = Sched(nc, es)
        PE, ACT, DVE, POOL, SP = S.pe, S.act, S.dve, S.pool, S.sp

        def const(name, shape, dt):
            return T(es.enter_context(nc.sbuf_tensor("c_" + name, shape, dt))[:])

        R1 = Region(S, nc, es, "R1", 16 * SQ)
        R2 = Region(S, nc, es, "R2", 16 * SQ)
        R34_MAIN = max(32 * SQ, TB * 2080 + 64 + 28672 + 512)
        R34 = Region(S, nc, es, "R34", R34_MAIN + 16384)
        M = Region(S, nc, es, "M", 56 * 1024)
        ONESF = M.tile(M.view(49408, [128, 128], F32))
        ZEROF = M.tile(M.view(49920, [128, 128], F32))
        LAMV = M.tile(M.view(50432, [128, 256], F32))
        LTMP = M.tile(M.view(51456, [128, 128], F32))
        PS = es.enter_context(nc.psum_tensor("PS", [128, 8, 512], F32))
        BK = [T(PS[:, i, :]) for i in range(8)]

        def bank_bf(i):
            return PS[:, i, :].bitcast(BF16)

        IDENT = const("ident", [128, 128], BF16)
        TRI = const("tri", [128, 128], BF16)
        BLK = const("blk", [128, 128], BF16)
        MHALF = const("mhalf", [128, 8], F32)
        G1s = const("G1s", [128, KC, NB], F32)
        S1s = const("S1s", [128, KC, NB], F32)
        G2s = const("G2s", [128, KC, NB], F32)
        S2s = const("S2s", [128, KC, NB], F32)
        N1G = const("n1g", [128, KC], F32)
        N2G = const("n2g", [128, KC], F32)
        BADAT = const("badaT", [128, 48], F32)
        CW = const("cw", [128, 3, KC], F32)
        GQ = const("gq", [128, 1], F32)
        GK = const("gk", [128, 1], F32)
        SUBG = const("subg", [128, 1], F32)
        NEGLAM = const("neglam", [128, 1], F32)
        CT = const("ct", [128, KC, NB], F32)
        SCT = const("sct", [128, KC, NB], BF16)
        LS = const("ls", [128, 2], F32)
        SSQ = [const("ssq%d" % i, [128, 1], F32) for i in range(4)]
        EPSC = const("epsc", [128, 1], F32)
        JUNK = es.enter_context(nc.sbuf_tensor("junk", [128, 1024], BF16))

        dsetup = [S.dsem("dsu%d" % i) for i in range(10)]
        k = 0
        for t_, src in ((CT, cT_d), (BADAT, badaT_d), (N1G, n1g_d), (N2G, n2g_d), (CW, cw_d), (GQ, gq_d),
                        (GK, gk_d), (SUBG, subg_d)):
            S.dma(SP, t_.ap, src, dsetup[k], writes=[t_])
            k += 1
        S.dma(SP, LAMV.ap, lamv_d.partition_broadcast(128), dsetup[k], writes=[LAMV])

        S.op(POOL, lambda: nc.gpsimd.memset(ONESF.ap, 1.0), writes=[ONESF])
        S.op(POOL, lambda: nc.gpsimd.memset(MHALF.ap, -0.5), writes=[MHALF])
        S.op(POOL, lambda: nc.gpsimd.memset(EPSC.ap, EPS), writes=[EPSC])
        S.op(POOL, lambda: nc.gpsimd.affine_select(out=IDENT.ap, in_=ONESF.ap, pattern=[[-1, 128]],
                                                   compare_op=ALU.is_equal, fill=0.0, base=0,
                                                   channel_multiplier=1), reads=[ONESF], writes=[IDENT])
        S.op(POOL, lambda: nc.gpsimd.memset(ZEROF.ap, 0.0), writes=[ZEROF])
        S.op(POOL, lambda: nc.gpsimd.affine_select(out=TRI.ap, in_=ZEROF.ap, pattern=[[1, 128]],
                                                   compare_op=ALU.is_ge, fill=-30000.0, base=0,
                                                   channel_multiplier=-1), reads=[ZEROF], writes=[TRI])
        S.op(POOL, lambda: nc.gpsimd.memset(BLK.ap, 0.0), writes=[BLK])
        S.op(POOL, lambda: nc.gpsimd.memset(BLK.ap[0:64, 0:64], 1.0), writes=[BLK])
        S.op(POOL, lambda: nc.gpsimd.memset(BLK.ap[64:128, 64:128], 1.0), writes=[BLK])
        S.op(DVE, lambda: nc.vector.tensor_scalar(out=SUBG.ap, in0=SUBG.ap, scalar1=1.0 - LAMBDA_INIT,
                                                  scalar2=None, op0=ALU.mult), reads=[SUBG], writes=[SUBG])
        S.op(DVE, lambda: nc.vector.tensor_tensor(out=LTMP.ap[:, 0:64], in0=LAMV.ap[:, 0:64],
                                                  in1=LAMV.ap[:, 64:128], op=ALU.mult), reads=[LAMV], writes=[LTMP])
        S.op(DVE, lambda: nc.vector.tensor_tensor(out=LTMP.ap[:, 64:128], in0=LAMV.ap[:, 128:192],
                                                  in1=LAMV.ap[:, 192:256], op=ALU.mult), reads=[LAMV], writes=[LTMP])
        S.op(DVE, lambda: nc.vector.reduce_sum(out=LS.ap, in_=LTMP.ap.rearrange("p (a b) -> p a b", b=64),
                                               axis=AX.X), reads=[LTMP], writes=[LS])
        S.op(ACT, lambda: nc.scalar.activation(out=LS.ap, in_=LS.ap, func=AF.Exp), reads=[LS], writes=[LS])
        S.op(DVE, lambda: nc.vector.tensor_tensor(out=NEGLAM.ap, in0=LS.ap[:, 1:2], in1=LS.ap[:, 0:1],
                                                  op=ALU.subtract), reads=[LS], writes=[NEGLAM])
        S.op(DVE, lambda: nc.vector.tensor_scalar(out=NEGLAM.ap, in0=NEGLAM.ap, scalar1=-LAMBDA_INIT,
                                                  scalar2=None, op0=ALU.add), reads=[NEGLAM], writes=[NEGLAM])
        S.op(ACT, lambda: nc.scalar.activation(out=SCT.ap, in_=CT.ap, func=AF.Silu), reads=[CT], writes=[SCT])

        WADA = [M.tile(M.view(i * 16384, [128, KC, 1024], BF16)) for i in range(2)]
        for i_, t_ in enumerate(WADA):
            t_.dsem = S.dsem("dwada%d" % i_)
        BBC = M.tile(M.view(32768, [128, 1024], F32))
        BBC.dsem = S.dsem("dbbc")
        GST = [M.tile(M.view(36864 + i * 4096, [128, 1024], F32)) for i in range(2)]
        SCB = [M.tile(M.view(45056 + i * 2048, [128, KC, 128], BF16)) for i in range(2)]
        MODT = M.tile(M.view(49152, [128, KC, NB], F32))
        gst_ds = [S.dsem("dgst%d" % i) for i in range(2)]
        GBC_D = [[T(gbc_d[b, w]) for w in range(2)] for b in range(NB)]

        def load_wada(j, pos):
            t_ = WADA[pos % 2]
            for hf in range(2):
                S.dma(POOL, t_.ap[:, :, hf * 512:(hf + 1) * 512],
                      wada_v[:, :, j * 1024 + hf * 512: j * 1024 + (hf + 1) * 512], t_.dsem, writes=[t_])
            return t_

        nbk = [0]

        def nextbank():
            b = nbk[0] % 8
            nbk[0] += 1
            return b

        order = [0, 1, 3, 4, 2, 5]
        cur = load_wada(order[0], 0)
        gcount = 0
        for oi, j in enumerate(order):
            nxt = load_wada(order[oi + 1], oi + 1) if oi + 1 < len(order) else None
            if j in (0, 1, 3, 4):
                b_ = nextbank()
                for fcn in range(KC):
                    for kc in range(KC):
                        S.op(PE, lambda: nc.tensor.matmul(PS[:, b_, fcn * NB:(fcn + 1) * NB],
                                                          lhsT=cur.ap[:, kc, fcn * 128:(fcn + 1) * 128],
                                                          rhs=SCT.ap[:, kc, :], start=(kc == 0), stop=(kc == KC - 1),
                                                          skip_group_check=True),
                             reads=[cur, SCT], writes=[BK[b_]], signal=(kc == KC - 1))
                psv = PS[:, b_, 0:KC * NB].rearrange("p (a b) -> p a b", b=NB)
                bias = BADAT.ap[:, j * 8:(j + 1) * 8].unsqueeze(2).to_broadcast([128, KC, NB])
                dst = {0: S1s, 1: MODT, 3: S2s, 4: MODT}[j]
                S.op(DVE, lambda: nc.vector.tensor_tensor(out=dst.ap, in0=psv, in1=bias, op=ALU.add),
                     reads=[BK[b_], BADAT], writes=[dst])
                if j in (1, 4):
                    gt, ng = (G1s, N1G) if j == 1 else (G2s, N2G)
                    S.op(DVE, lambda: nc.vector.scalar_tensor_tensor(
                        out=gt.ap, in0=MODT.ap, scalar=1.0,
                        in1=ng.ap.unsqueeze(2).to_broadcast([128, KC, NB]), op0=ALU.add, op1=ALU.mult),
                        reads=[MODT, ng], writes=[gt])
            else:
                w_ = 0 if j == 2 else 1
                S.dma(SP, BBC.ap, badar_d[:, j * 1024:(j + 1) * 1024].partition_broadcast(128), BBC.dsem,
                      writes=[BBC])
                for b in range(NB):
                    scb = SCB[gcount % 2]
                    gst = GST[gcount % 2]
                    S.op(DVE, lambda: nc.vector.tensor_copy(
                        out=scb.ap, in_=SCT.ap[:, :, b:b + 1].to_broadcast([128, KC, 128])),
                        reads=[SCT], writes=[scb])
                    for hf in range(2):
                        b_ = nextbank()
                        for kc in range(KC):
                            S.op(PE, lambda: nc.tensor.matmul(PS[:, b_, :], lhsT=scb.ap[:, kc, :],
                                                              rhs=cur.ap[:, kc, hf * 512:(hf + 1) * 512],
                                                              start=(kc == 0), stop=(kc == KC - 1)),
                                 reads=[scb, cur], writes=[BK[b_]], signal=(kc == KC - 1))
                        S.op(DVE, lambda: nc.vector.tensor_tensor(out=gst.ap[:, hf * 512:(hf + 1) * 512],
                                                                  in0=PS[:, b_, :],
                                                                  in1=BBC.ap[:, hf * 512:(hf + 1) * 512],
                                                                  op=ALU.add),
                             reads=[BK[b_], BBC], writes=[gst])
                    S.dma(SP, gbc_d[b, w_], gst.ap, gst_ds[gcount % 2], reads=[gst], writes=[GBC_D[b][w_]])
                    gcount += 1
            cur = nxt
        M.reset()

        out_ds = [S.dsem("dout%d" % i) for i in range(TB)]
        dbg_ds = S.dsem("ddbg")

        def dump(name, ap, tiles, dt=BF16):
            if not debug:
                return
            d_ = nc.dram_tensor("dbg_" + name, list(ap.shape), dt, kind="ExternalOutput").ap()
            S.dma(SP, d_, ap, dbg_ds, reads=tiles)

        for nm, t_ in (("G1s", G1s), ("S1s", S1s), ("G2s", G2s), ("S2s", S2s), ("NEGLAM", NEGLAM), ("SUBG", SUBG)):
            dump(nm, t_.ap, [t_], F32)
        xin_ds = [S.dsem("dxin%d" % i) for i in range(8)]
        wv_ds = [S.dsem("dwv%d" % i) for i in range(2)]
        wq_ds = [S.dsem("dwq%d" % i) for i in range(4)]
        wc_ds = [S.dsem("dwc%d" % i) for i in range(6)]
        wd_ds = [S.dsem("dwd%d" % i) for i in range(8)]
        wo_ds = S.dsem("dwo")
        wg_ds = [S.dsem("dwg%d" % i) for i in range(2)]
        wdn_ds = [S.dsem("dwdn%d" % i) for i in range(2)]
        g_ds = [S.dsem("dg%d" % i) for i in range(2)]

        def evac_mod(idx, out_ap, in_ap, g_ap, s_ap, reads, writes):
            if idx % 2 == 0:
                S.op(ACT, lambda: nc.scalar.activation(out=out_ap, in_=in_ap, func=AF.Identity, scale=g_ap,
                                                       bias=s_ap), reads=reads, writes=writes)
            else:
                S.op(DVE, lambda: nc.vector.tensor_scalar(out=out_ap, in0=in_ap, scalar1=g_ap, scalar2=s_ap,
                                                          op0=ALU.mult, op1=ALU.add), reads=reads, writes=writes)

        def rstd_from_ssq(ssq, n):
            S.op(DVE, lambda: nc.vector.tensor_scalar(out=ssq.ap, in0=ssq.ap, scalar1=1.0 / n, scalar2=EPS,
                                                      op0=ALU.mult, op1=ALU.add), reads=[ssq], writes=[ssq])
            S.op(POOL, lambda: nc.gpsimd.tensor_tensor(out=ssq.ap, in0=ssq.ap, in1=MHALF.ap[:, 0:1], op=ALU.pow),
                 reads=[ssq, MHALF], writes=[ssq])

        def norm_tile(src_t, src_ap, ssq, xn):
            S.op(ACT, lambda: nc.scalar.activation(out=JUNK[:, :], in_=src_ap, func=AF.Square,
                                                   accum_out=ssq.ap), reads=[src_t], writes=[ssq])
            rstd_from_ssq(ssq, D)
            S.op(DVE, lambda: nc.vector.tensor_scalar(out=xn.ap, in0=src_ap, scalar1=ssq.ap, scalar2=None,
                                                      op0=ALU.mult), reads=[src_t, ssq], writes=[xn])
            return xn

        def transpose_group(xns, dstT, Gs, Ss, b, col0, bank_base):
            for kc in range(KC):
                b_ = bank_base + kc // 2
                pv = bank_bf(b_)
                c0 = (kc % 2) * 512
                for i in range(4):
                    S.op(PE, lambda: nc.tensor.transpose(pv[:, c0 + i * 128: c0 + (i + 1) * 128],
                                                         xns[i].ap[:, kc * 128:(kc + 1) * 128], IDENT.ap),
                         reads=[xns[i], IDENT], writes=[BK[b_]], signal=(i == 3))
                evac_mod(kc, dstT[kc][0][:, col0:col0 + 512], pv[:, c0:c0 + 512], Gs.ap[:, kc, b:b + 1],
                         Ss.ap[:, kc, b:b + 1], [BK[b_], Gs, Ss], [dstT[kc][1][col0 // 512]])

        def norm_transpose(src_tiles, XN, dstT, Gs, Ss, b, tg_global, col0):
            xns = [norm_tile(src_t, src_ap, SSQ[i], XN[(tg_global * 4 + i) % len(XN)])
                   for i, (src_t, src_ap) in enumerate(src_tiles)]
            transpose_group(xns, dstT, Gs, Ss, b, col0, (tg_global % 2) * 4)

        def load_w(t_, dst_ap, src_ap):
            S.dma(POOL, dst_ap, src_ap, t_.dsem, writes=[t_])

        def mmgroup(out_ap, pairs, bk, reads, **kw):
            n = len(pairs)
            for i, (l, r) in enumerate(pairs):
                S.op(PE, lambda: nc.tensor.matmul(out_ap, lhsT=l, rhs=r, start=(i == 0), stop=(i == n - 1), **kw),
                     reads=reads, writes=[bk], signal=(i == n - 1))

        tgc = [0]
        ecount = [0]
        WV = [T(R34.view(R34_MAIN + i * 8192, [128, KC, 512], BF16)) for i in range(2)]
        for i in range(2):
            WV[i].dsem = wv_ds[i]
            load_w(WV[i], WV[i].ap, win_v[:, :, 2048 + i * 512: 2048 + (i + 1) * 512])

        for b in range(NB):
            R1.reset()
            M.reset()
            hT_ap = R1.view(0, [128, KC, SQ], BF16)
            hT = [(hT_ap[:, kc, :], [R1.tile(hT_ap[:, kc, g * 512:(g + 1) * 512]) for g in range(TG)])
                  for kc in range(KC)]
            XIN = [M.tile(M.view(i * 4096, [128, 1024], F32)) for i in range(8)]
            for i in range(8):
                XIN[i].dsem = xin_ds[i]
            XN = [M.tile(M.view(32768 + i * 2048, [128, 1024], BF16)) for i in range(4)]
            R34.reset()
            R2.reset()
            VA_ap = R34.view(0, [128, TB, H, 130], BF16)
            VA = [[R34.tile(VA_ap[:, tb, hf * 4:(hf + 1) * 4, :]) for hf in range(2)] for tb in range(TB)]
            allva = [VA[tb][hf] for tb in range(TB) for hf in range(2)]
            S.op(POOL, lambda: nc.gpsimd.memset(VA_ap[:, :, :, 128:130], 1.0), writes=allva)

            vcnt = [0]

            def v_proj(tb):
                for hf in range(2):
                    b_ = 4 + vcnt[0] % 4
                    vcnt[0] += 1
                    mmgroup(PS[:, b_, :], [(hT[kc][0][:, tb * 128:(tb + 1) * 128], WV[hf].ap[:, kc, :])
                                           for kc in range(KC)], BK[b_],
                            [hT[kc][1][tb // 4] for kc in range(KC)] + [WV[hf]])
                    src = PS[:, b_, :].rearrange("p (a b) -> p a b", b=128)
                    dst = VA_ap[:, tb, hf * 4:(hf + 1) * 4, 0:128]
                    if hf == 0:
                        S.op(ACT, lambda: nc.scalar.copy(out=dst, in_=src), reads=[BK[b_]], writes=[VA[tb][hf]])
                    else:
                        S.op(DVE, lambda: nc.vector.tensor_copy(out=dst, in_=src), reads=[BK[b_]],
                             writes=[VA[tb][hf]])

            def load_x(tg):
                for i in range(4):
                    tb = tg * 4 + i
                    xin = XIN[(tg % 2) * 4 + i]
                    S.dma(SP, xin.ap, x_d[b, tb * 128:(tb + 1) * 128, :], xin.dsem, writes=[xin])

            load_x(0)
            for tg in range(TG):
                if tg + 1 < TG:
                    load_x(tg + 1)
                xns = []
                for i in range(4):
                    xin = XIN[(tg % 2) * 4 + i]
                    xns.append(norm_tile(xin, xin.ap, SSQ[i], XN[i]))
                    if tg >= 1:
                        v_proj((tg - 1) * 4 + i)
                transpose_group(xns, hT, G1s, S1s, b, tg * 512, 0)
            for i in range(4):
                v_proj((TG - 1) * 4 + i)

            oT_ap = R2.view(0, [128, H, SQ], BF16)
            oT = [[R2.tile(oT_ap[:, h, g * 512:(g + 1) * 512]) for g in range(TG)] for h in range(H)]
            M.reset()
            WQ = [M.tile(M.view(i * 4096, [128, KC, 256], BF16)) for i in range(2)]
            WK = [M.tile(M.view(8192 + i * 4096, [128, KC, 256], BF16)) for i in range(2)]
            for i in range(2):
                WQ[i].dsem = wq_ds[i]
                WK[i].dsem = wq_ds[2 + i]
            QT = [M.view(16384 + i * 2 * SQ, [128, SQ], BF16) for i in range(2)]
            KT = [M.view(16384 + 4 * SQ + i * 2 * SQ, [128, SQ], BF16) for i in range(2)]
            QTt = [[M.tile(QT[i][:, g * 512:(g + 1) * 512]) for g in range(TG)] for i in range(2)]
            KTt = [[M.tile(KT[i][:, g * 512:(g + 1) * 512]) for g in range(TG)] for i in range(2)]
            o = 16384 + 8 * SQ
            ET = [M.tile(M.view(o + i * 2048, [128, 2, 512], BF16)) for i in range(3)]
            o += 6144
            o34 = (TB * 2080 + 63) // 64 * 64
            NQS = 8
            QS = [R34.tile(R34.view(o34 + i * 2048, [128, 512], F32)) for i in range(NQS)]
            o34 += 2048 * NQS
            SQB = [R34.tile(R34.view(o34 + i * 1024, [128, 512], BF16)) for i in range(NQS)]
            o34 += 1024 * NQS
            SDB = [R34.tile(R34.view(o34 + i * 2048, [128, 512], F32)) for i in range(2)]
            o34 += 4096
            RC = [M.tile(M.view(o + i * 32, [128, 8], F32)) for i in range(2)]
            o += 64
            SS4 = [M.tile(M.view(o + i * 32, [128, 4], F32)) for i in range(2)]
            o += 64
            AS = [M.tile(M.view(o + i * 4128, [128, 8, 129], F32)) for i in range(2)]
            o += 8256
            ON = [M.tile(M.view(o + i * 1024, [128, 4, 128], BF16)) for i in range(2)]
            o += 2048
            assert o <= M.nbytes, o

            def load_qk(hp):
                sl = hp % 2
                load_w(WQ[sl], WQ[sl].ap, win_v[:, :, hp * 256:(hp + 1) * 256])
                load_w(WK[sl], WK[sl].ap, win_v[:, :, 1024 + hp * 256: 1024 + (hp + 1) * 256])

            load_qk(0)
            nq = [0]
            acc_slots = [((4 + s_, il * 129) if il < 3 else (6, s_ * 129)) for il in range(4) for s_ in range(2)]
            deferred = []
            nchunk = [0]

            def flush_deferred():
                while deferred:
                    deferred.pop(0)()
            tiles_ = [(tc, w) for tc in range(TG) for w in range(2)]

            def make_units(h_):
                hp_, par_ = h_ // 2, h_ % 2

                def unit_a(t, b_, half=None):
                    tc, w = tiles_[t]
                    W_ = (WQ, WK)[w][hp_ % 2]
                    kcs = range(KC) if half is None else range(half * 4, half * 4 + 4)
                    for kc in kcs:
                        S.op(PE, lambda: nc.tensor.matmul(PS[:, b_, :],
                                                          lhsT=W_.ap[:, kc, (h_ % 2) * 128:(h_ % 2 + 1) * 128],
                                                          rhs=hT[kc][0][:, tc * 512:(tc + 1) * 512],
                                                          start=(kc == 0), stop=(kc == KC - 1)),
                             reads=[hT[kc][1][tc], W_], writes=[BK[b_]], signal=(kc == KC - 1))
                    if half == 0:
                        return
                    sq, qs = SQB[t % NQS], QS[t % NQS]
                    S.op(DVE, lambda: nc.vector.tensor_copy(out=qs.ap, in_=PS[:, b_, :]), reads=[BK[b_]],
                         writes=[qs])
                    E_, e_ = (POOL, nc.gpsimd) if POOL_OFFLOAD else (DVE, nc.vector)
                    S.op(E_, lambda: e_.tensor_tensor(out=sq.ap, in0=qs.ap, in1=qs.ap, op=ALU.mult),
                         reads=[qs], writes=[sq])

                def unit_b(t, b2):
                    tc, w = tiles_[t]
                    gvec = (GQ, GK)[w]
                    dst_ap = (QT, KT)[w][par_]
                    dst_t = (QTt, KTt)[w][par_]
                    sq, qs, sd = SQB[t % NQS], QS[t % NQS], SDB[t % 2]
                    S.op(PE, lambda: nc.tensor.matmul(PS[:, b2, :], lhsT=BLK.ap, rhs=sq.ap, start=True,
                                                      stop=True), reads=[BLK, sq], writes=[BK[b2]])
                    S.op(ACT, lambda: nc.scalar.activation(out=sd.ap, in_=PS[:, b2, :], func=AF.Ln,
                                                           scale=1.0 / 64, bias=EPSC.ap),
                         reads=[BK[b2], EPSC], writes=[sd])
                    S.op(ACT, lambda: nc.scalar.activation(out=sd.ap, in_=sd.ap, func=AF.Exp, scale=-0.5),
                         reads=[sd], writes=[sd])
                    S.op(DVE, lambda: nc.vector.scalar_tensor_tensor(
                        out=dst_ap[:, tc * 512:(tc + 1) * 512], in0=qs.ap, scalar=gvec.ap, in1=sd.ap,
                        op0=ALU.mult, op1=ALU.mult), reads=[qs, gvec, sd], writes=[dst_t[tc]])

                return unit_a, unit_b

            def proj_standalone(h_):
                ua0, ub0 = make_units(h_)
                for t in range(len(tiles_)):
                    ua0(t, t % 4)
                    if t >= 2:
                        ub0(t - 2, 4 + t % 2)
                for t in range(max(0, len(tiles_) - 2), len(tiles_)):
                    ub0(t, 4 + t % 2)

            proj_standalone(0)

            for h in range(H):
                hp = h // 2
                if h % 2 == 0 and hp + 1 < H // 2:
                    load_qk(hp + 1)
                par = h % 2
                steps = [(qc, j) for qc in range(TG) for j in range(4 * qc + 4)]
                qt, kt, qtt, ktt = QT[par], KT[par], QTt[par], KTt[par]

                def s_stage(si):
                    qc, j = steps[si]
                    il0 = max(0, j - 4 * qc)
                    ncols = 512 - il0 * 128
                    q0 = qc * 512 + il0 * 128
                    b0 = 2 * (si % 2)
                    et = ET[si % 3]
                    diag = j >= 4 * qc
                    for s_ in range(2):
                        S.op(PE, lambda: nc.tensor.matmul(PS[:, b0 + s_, 0:ncols],
                                                          lhsT=kt[s_ * 64:(s_ + 1) * 64, j * 128:(j + 1) * 128],
                                                          rhs=qt[s_ * 64:(s_ + 1) * 64, q0:q0 + ncols],
                                                          start=True, stop=not diag),
                             reads=[ktt[j // 4], qtt[qc]], writes=[BK[b0 + s_]], signal=not diag)
                    if diag:
                        for s_ in range(2):
                            S.op(PE, lambda: nc.tensor.matmul(PS[:, b0 + s_, 0:128], lhsT=IDENT.ap, rhs=TRI.ap,
                                                              start=False, stop=True),
                                 reads=[IDENT, TRI], writes=[BK[b0 + s_]])
                    S.op(ACT, lambda: nc.scalar.activation(out=et.ap[:, :, 0:ncols], in_=PS[:, b0:b0 + 2, 0:ncols],
                                                           func=AF.Exp, scale=0.125),
                         reads=[BK[b0], BK[b0 + 1]], writes=[et])

                def av_stage(si):
                    qc, j = steps[si]
                    il0 = max(0, j - 4 * qc)
                    et = ET[si % 3]
                    mms = []
                    for il in range(il0, 4):
                        for s_ in range(2):
                            bk_, off = acc_slots[il * 2 + s_]
                            mms.append((bk_, off, il, s_))
                    for n_, (bk_, off, il, s_) in enumerate(mms):
                        c0 = (il - il0) * 128
                        S.op(PE, lambda: nc.tensor.matmul(PS[:, bk_, off:off + 129],
                                                          lhsT=et.ap[:, s_, c0:c0 + 128],
                                                          rhs=VA_ap[:, j, h, 0:129],
                                                          start=(j == 0 and off == 0), stop=(j == 4 * qc + il),
                                                          skip_group_check=True),
                             reads=[et, VA[j][h // 4]], writes=[BK[bk_]], signal=(n_ == len(mms) - 1))
                    if j == 4 * qc + 3:
                        evac_chunk(qc)

                def evac_chunk(qc, h=h):
                    k_ = nchunk[0] % 2
                    nchunk[0] += 1
                    as_, rc, ss, on = AS[k_], RC[k_], SS4[k_], ON[k_]
                    as4 = as_.ap.rearrange("p (s i) c -> p s i c", i=4)
                    S.op(DVE, lambda: nc.vector.tensor_copy(
                        out=as4[:, :, 0:3, :],
                        in_=PS[:, 4:6, 0:387].rearrange("p s (a c) -> p s a c", c=129)),
                        reads=[BK[4], BK[5]], writes=[as_])
                    S.op(DVE, lambda: nc.vector.tensor_copy(
                        out=as4[:, :, 3, :], in_=PS[:, 6, 0:258].rearrange("p (a c) -> p a c", c=129)),
                        reads=[BK[6]], writes=[as_])
                    if boundary_units:
                        boundary_units.pop(0)()
                    flush_deferred()
                    S.op(DVE, lambda: nc.vector.reciprocal(out=rc.ap, in_=as_.ap[:, :, 128]), reads=[as_],
                         writes=[rc])
                    S.op(DVE, lambda: nc.vector.tensor_scalar(out=rc.ap[:, 4:8], in0=rc.ap[:, 4:8],
                                                              scalar1=NEGLAM.ap, scalar2=None, op0=ALU.mult),
                         reads=[rc, NEGLAM], writes=[rc])
                    E_, e_ = (POOL, nc.gpsimd) if POOL_OFFLOAD else (DVE, nc.vector)
                    S.op(E_, lambda: e_.tensor_tensor(
                        out=as_.ap[:, :, 0:128], in0=as_.ap[:, :, 0:128],
                        in1=rc.ap.unsqueeze(2).to_broadcast([128, 8, 128]), op=ALU.mult),
                        reads=[as_, rc], writes=[as_])
                    S.op(E_, lambda: e_.tensor_tensor(out=as_.ap[:, 0:4, 0:128], in0=as_.ap[:, 0:4, 0:128],
                                                      in1=as_.ap[:, 4:8, 0:128], op=ALU.add),
                         reads=[as_], writes=[as_])
                    S.op(DVE, lambda: nc.vector.tensor_tensor(out=as_.ap[:, 4:8, 0:128], in0=as_.ap[:, 0:4, 0:128],
                                                              in1=as_.ap[:, 0:4, 0:128], op=ALU.mult),
                         reads=[as_], writes=[as_])
                    S.op(DVE, lambda: nc.vector.reduce_sum(out=ss.ap, in_=as_.ap[:, 4:8, 0:128], axis=AX.X),
                         reads=[as_], writes=[ss])
                    S.op(DVE, lambda: nc.vector.tensor_scalar(out=ss.ap, in0=ss.ap, scalar1=1.0 / 128, scalar2=EPS,
                                                              op0=ALU.mult, op1=ALU.add), reads=[ss], writes=[ss])
                    S.op(POOL, lambda: nc.gpsimd.tensor_tensor(out=ss.ap, in0=ss.ap, in1=MHALF.ap[:, 0:4],
                                                               op=ALU.pow), reads=[ss, MHALF], writes=[ss])
                    S.op(POOL, lambda: nc.gpsimd.tensor_tensor(
                        out=on.ap, in0=as_.ap[:, 0:4, 0:128], in1=ss.ap.unsqueeze(2).to_broadcast([128, 4, 128]),
                        op=ALU.mult), reads=[as_, ss], writes=[on])

                    def fin():
                        for il in range(4):
                            S.op(PE, lambda: nc.tensor.transpose(bank_bf(7)[:, il * 128:(il + 1) * 128],
                                                                 on.ap[:, il, :], IDENT.ap),
                                 reads=[on, IDENT], writes=[BK[7]], signal=(il == 3))
                        S.op(DVE, lambda: nc.vector.tensor_scalar(out=oT_ap[:, h, qc * 512:(qc + 1) * 512],
                                                                  in0=bank_bf(7)[:, 0:512], scalar1=SUBG.ap,
                                                                  scalar2=None, op0=ALU.mult),
                             reads=[BK[7], SUBG], writes=[oT[h][qc]])
                    deferred.append(fin)

                sched_ = {}
                boundary_units = []
                if h + 1 < H and INTERLEAVE_PROJ:
                    ua, ub = make_units(h + 1)
                    sp_ = max(2, len(steps) // len(tiles_))
                    nb_ = min(TG, len(tiles_))
                    for t in range(nb_):
                        boundary_units.append(lambda t=t, ua=ua: ua(t, 7))
                    rest = list(range(nb_, len(tiles_)))
                    slots_ = []
                    for qc_ in range(1, TG):
                        base = sum(4 * q_ + 4 for q_ in range(qc_))
                        ln_ = 4 * qc_ + 4
                        npos = 1 if qc_ < TG - 1 else max(1, len(rest) - (TG - 2))
                        for p_ in range(npos):
                            slots_.append(base + (p_ + 1) * ln_ // (npos + 1) - 1)
                    for t, st_ in zip(rest, slots_):
                        sched_.setdefault(st_, []).append(lambda t=t, ua=ua: ua(t, 7, 0))
                        sched_.setdefault(st_ + 1, []).append(lambda t=t, ua=ua: ua(t, 7, 1))
                    for t in rest[len(slots_):]:
                        sched_.setdefault(10 ** 6 + t, []).append(lambda t=t, ua=ua: ua(t, 7))
                flush_deferred()
                s_stage(0)
                if len(steps) > 1:
                    s_stage(1)
                for si in range(len(steps)):
                    if si + 2 < len(steps):
                        s_stage(si + 2)
                    av_stage(si)
                    for u_ in sched_.pop(si, []):
                        u_()
                while boundary_units:
                    boundary_units.pop(0)()
                for k_ in sorted(sched_):
                    for u_ in sched_[k_]:
                        u_()
                if h + 1 < H and not INTERLEAVE_PROJ:
                    proj_standalone(h + 1)
                if h + 1 < H and INTERLEAVE_PROJ:
                    for t in range(len(tiles_)):
                        ub(t, 4 + t % 2)

            flush_deferred()
            if b == 0:
                dump("VA", VA_ap, allva)
                dump("QT", QT[1], QTt[1])
                dump("KT", KT[1], KTt[1])
                dump("oT", oT_ap, [t_ for h_ in range(H) for t_ in oT[h_]])
            R34.reset()
            M.reset()
            ya_ap = R34.view(0, [128, KC, SQ], BF16)
            yaT = [[R34.tile(ya_ap[:, f, g * 512:(g + 1) * 512]) for g in range(TG)] for f in range(KC)]
            mT_ap = R34.view(16 * SQ, [128, KC, SQ], BF16)
            mT = [[R34.tile(mT_ap[:, f, g * 512:(g + 1) * 512]) for g in range(TG)] for f in range(KC)]
            WC = [[M.tile(M.view((i * 3 + j) * 4096, [128, KC, 256], BF16)) for j in range(3)] for i in range(2)]
            for i in range(2):
                for j in range(3):
                    WC[i][j].dsem = wc_ds[i * 3 + j]
            o = 24576
            U_ap = M.view(o, [128, 2 + SQ], F32)
            Ut = [M.tile(U_ap[:, 2 + g * 512: 2 + (g + 1) * 512]) for g in range(TG)]
            Upad = M.tile(U_ap[:, 0:2])
            o += (2 + SQ) * 4
            o = (o + 31) // 32 * 32
            CCS = [M.tile(M.view(o + i * 2048, [128, 512], F32)) for i in range(2)]
            o += 4096
            YB = [M.tile(M.view(o + i * 2048, [128, 512], F32)) for i in range(2)]
            o += 4096
            assert o <= M.nbytes, o
            S.op(POOL, lambda: nc.gpsimd.memset(U_ap[:, 0:2], 0.0), writes=[Upad])

            def load_conv(fp):
                sl = fp % 2
                for j, base in enumerate((4096, 5120, 3072)):
                    load_w(WC[sl][j], WC[sl][j].ap, win_v[:, :, base + fp * 256: base + (fp + 1) * 256])

            load_conv(0)
            cidx = 0
            for f in range(KC):
                fp = f // 2
                if f % 2 == 0 and fp + 1 < KC // 2:
                    load_conv(fp + 1)
                Wcc, Wcx, Wcb = WC[fp % 2]
                for tc in range(TG):
                    bks = [(cidx % 2) * 3 + i for i in range(3)]
                    for W_, b_ in zip((Wcc, Wcx, Wcb), bks):
                        mmgroup(PS[:, b_, :], [(W_.ap[:, kc, (f % 2) * 128:(f % 2 + 1) * 128],
                                                hT[kc][0][:, tc * 512:(tc + 1) * 512]) for kc in range(KC)],
                                BK[b_], [hT[kc][1][tc] for kc in range(KC)] + [W_])
                    ccs = CCS[cidx % 2]
                    yb = YB[cidx % 2]
                    S.op(ACT, lambda: nc.scalar.copy(out=ccs.ap, in_=PS[:, bks[0], :]), reads=[BK[bks[0]]],
                         writes=[ccs])
                    ut = Ut[tc]
                    S.op(DVE, lambda: nc.vector.tensor_tensor(out=ut.ap, in0=PS[:, bks[1], :], in1=ccs.ap,
                                                              op=ALU.mult), reads=[BK[bks[1]], ccs], writes=[ut])
                    prev = [Ut[tc - 1]] if tc > 0 else [Upad]
                    c0 = 2 + tc * 512
                    S.op(POOL, lambda: nc.gpsimd.tensor_scalar(out=yb.ap, in0=U_ap[:, c0 - 2:c0 - 2 + 512],
                                                               scalar1=CW.ap[:, 0, f:f + 1], scalar2=0.0,
                                                               op0=ALU.mult, op1=ALU.add),
                         reads=[ut, CW] + prev, writes=[yb])
                    S.op(DVE, lambda: nc.vector.scalar_tensor_tensor(out=yb.ap, in0=U_ap[:, c0 - 1:c0 - 1 + 512],
                                                                      scalar=CW.ap[:, 1, f:f + 1], in1=yb.ap,
                                                                      op0=ALU.mult, op1=ALU.add),
                         reads=[ut, CW, yb] + prev, writes=[yb])
                    S.op(DVE, lambda: nc.vector.scalar_tensor_tensor(out=yb.ap, in0=U_ap[:, c0:c0 + 512],
                                                                      scalar=CW.ap[:, 2, f:f + 1], in1=yb.ap,
                                                                      op0=ALU.mult, op1=ALU.add),
                         reads=[ut, CW, yb], writes=[yb])
                    S.op(DVE, lambda: nc.vector.tensor_tensor(out=ya_ap[:, f, tc * 512:(tc + 1) * 512],
                                                              in0=PS[:, bks[2], :], in1=yb.ap, op=ALU.mult),
                         reads=[BK[bks[2]], yb], writes=[yaT[f][tc]])
                    cidx += 1

            M.reset()
            WD4 = [[M.tile(M.view((i * 4 + j) * 4096, [128, KC, 256], BF16)) for j in range(4)] for i in range(2)]
            for i in range(2):
                for j in range(4):
                    WD4[i][j].dsem = wd_ds[i * 4 + j]
            o = 32768
            SG = [[M.tile(M.view(o + (i * 2 + j) * 2048, [128, 512], F32)) for j in range(2)] for i in range(2)]
            o += 8192
            M1 = [[M.tile(M.view(o + (i * 2 + j) * 2048, [128, 512], F32)) for j in range(2)] for i in range(2)]
            o += 8192
            assert o <= M.nbytes

            def load_d(fp):
                sl = fp % 2
                sl_c = slice(fp * 256, (fp + 1) * 256)
                load_w(WD4[sl][0], WD4[sl][0].ap, wa_v[:, :, sl_c])
                load_w(WD4[sl][1], WD4[sl][1].ap, wb_v[:, :, sl_c])
                load_w(WD4[sl][2], WD4[sl][2].ap, win_v[:, :, 6144 + fp * 256: 6144 + (fp + 1) * 256])
                load_w(WD4[sl][3], WD4[sl][3].ap, win_v[:, :, 7168 + fp * 256: 7168 + (fp + 1) * 256])

            load_d(0)
            didx = 0
            for f in range(KC):
                fp = f // 2
                if f % 2 == 0 and fp + 1 < KC // 2:
                    load_d(fp + 1)
                Wa_, Wb_, Wga_, Wgb_ = WD4[fp % 2]
                cs = slice((f % 2) * 128, (f % 2 + 1) * 128)
                for tc in range(TG):
                    ts = slice(tc * 512, (tc + 1) * 512)
                    bks = [(didx % 2) * 4 + i for i in range(4)]
                    mmgroup(PS[:, bks[0], :], [(Wga_.ap[:, kc, cs], hT[kc][0][:, ts]) for kc in range(KC)],
                            BK[bks[0]], [hT[kc][1][tc] for kc in range(KC)] + [Wga_])
                    mmgroup(PS[:, bks[1], :], [(Wgb_.ap[:, kc, cs], hT[kc][0][:, ts]) for kc in range(KC)],
                            BK[bks[1]], [hT[kc][1][tc] for kc in range(KC)] + [Wgb_])
                    mmgroup(PS[:, bks[2], :], [(Wa_.ap[:, kc, cs], ya_ap[:, kc, ts]) for kc in range(KC)],
                            BK[bks[2]], [yaT[kc][tc] for kc in range(KC)] + [Wa_])
                    mmgroup(PS[:, bks[3], :], [(Wb_.ap[:, kc, cs], oT_ap[:, kc, ts]) for kc in range(KC)],
                            BK[bks[3]], [oT[kc][tc] for kc in range(KC)] + [Wb_])
                    sga, sgb = SG[didx % 2]
                    m1, m2 = M1[didx % 2]
                    S.op(ACT, lambda: nc.scalar.activation(out=sga.ap, in_=PS[:, bks[0], :], func=AF.Sigmoid),
                         reads=[BK[bks[0]]], writes=[sga])
                    S.op(ACT, lambda: nc.scalar.activation(out=sgb.ap, in_=PS[:, bks[1], :], func=AF.Sigmoid),
                         reads=[BK[bks[1]]], writes=[sgb])
                    S.op(DVE, lambda: nc.vector.tensor_tensor(out=m1.ap, in0=PS[:, bks[2], :], in1=sga.ap,
                                                              op=ALU.mult), reads=[BK[bks[2]], sga], writes=[m1])
                    S.op(DVE, lambda: nc.vector.tensor_tensor(out=m2.ap, in0=PS[:, bks[3], :], in1=sgb.ap,
                                                              op=ALU.mult), reads=[BK[bks[3]], sgb], writes=[m2])
                    S.op(POOL, lambda: nc.gpsimd.tensor_tensor(out=mT_ap[:, f, ts], in0=m1.ap, in1=m2.ap,
                                                               op=ALU.add), reads=[m1, m2], writes=[mT[f][tc]])
                    didx += 1

            if b == 0:
                dump("ya", ya_ap, [t_ for f_ in range(KC) for t_ in yaT[f_]])
                dump("mT", mT_ap, [t_ for f_ in range(KC) for t_ in mT[f_]])
            R1.reset()
            R2.reset()
            M.reset()
            x1a = R1.view(0, [128, TB // 2, D], F32)
            x1b = R2.view(0, [128, TB // 2, D], F32)

            def x1_ap(tb):
                return (x1a if tb < TB // 2 else x1b)[:, tb % (TB // 2), :]

            X1 = [(R1 if tb < TB // 2 else R2).tile(x1_ap(tb)) for tb in range(TB)]
            ya_dead = S.retire([yaT[f][g] for f in range(KC) for g in range(TG)])
            WO = T(R34.view(0, [128, KC, D], BF16), ya_dead)
            WO.dsem = wo_ds
            R34.tiles.append(WO)
            for hf in range(2):
                load_w(WO, WO.ap[:, :, hf * 512:(hf + 1) * 512], wo_v[:, :, hf * 512:(hf + 1) * 512])
            XIN = [M.tile(M.view(i * 4096, [128, 1024], F32)) for i in range(2)]
            for i in range(2):
                XIN[i].dsem = xin_ds[i]
            TMP = [M.tile(M.view(8192 + i * 2048, [128, 512], F32)) for i in range(2)]
            G1BC = M.tile(M.view(12288, [128, D], F32))
            G1BC.dsem = g_ds[0]
            S.dma(SP, G1BC.ap, gbc_d[b, 0], G1BC.dsem, reads=[GBC_D[b][0]], writes=[G1BC])
            eidx = 0
            for tb in range(TB):
                xin = XIN[tb % 2]
                S.dma(SP, xin.ap, x_d[b, tb * 128:(tb + 1) * 128, :], xin.dsem, writes=[xin])
                for hf in range(2):
                    b_ = nextbank()
                    hs = slice(hf * 512, (hf + 1) * 512)
                    mmgroup(PS[:, b_, :], [(mT_ap[:, kc, tb * 128:(tb + 1) * 128], WO.ap[:, kc, hs])
                                           for kc in range(KC)], BK[b_],
                            [mT[kc][tb // 4] for kc in range(KC)] + [WO])
                    tmp = TMP[eidx % 2]
                    S.op(DVE, lambda: nc.vector.tensor_tensor(out=tmp.ap, in0=PS[:, b_, :], in1=G1BC.ap[:, hs],
                                                              op=ALU.mult), reads=[BK[b_], G1BC], writes=[tmp])
                    S.op(POOL, lambda: nc.gpsimd.tensor_tensor(out=x1_ap(tb)[:, hs], in0=tmp.ap, in1=xin.ap[:, hs],
                                                               op=ALU.add), reads=[tmp, xin], writes=[X1[tb]])
                    eidx += 1

            if b == 0:
                dump("x1a", x1a, X1[:TB // 2], F32)
                dump("x1b", x1b, X1[TB // 2:], F32)
            R34.reset()
            M.reset()
            TMP = [M.tile(M.view(i * 1024, [128, 256], F32)) for i in range(2)]
            SGB_ = [M.tile(M.view(2048 + i * 2048, [128, 512], F32)) for i in range(2)]
            G2BC = M.tile(M.view(6144, [128, D], F32))
            G2BC.dsem = g_ds[1]
            S.dma(SP, G2BC.ap, gbc_d[b, 1], G2BC.dsem, reads=[GBC_D[b][1]], writes=[G2BC])
            XN = [M.tile(M.view(10240 + i * 2048, [128, 1024], BF16)) for i in range(4)]
            WG = [M.tile(M.view(18432 + i * 8192, [128, KC, 2, 256], BF16)) for i in range(2)]
            for i in range(2):
                WG[i].dsem = wg_ds[i]
            o = 18432 + 16384
            WDN = [M.tile(M.view(o + i * 11264, [128, FC, 256], BF16)) for i in range(2)]
            for i in range(2):
                WDN[i].dsem = wdn_ds[i]
            o += 22528
            assert o <= M.nbytes, o
            NCH = SQ // HS
            h2_ap = R34.view(0, [128, KC, HS], BF16)
            aT_ap = R34.view(16 * HS, [128, FC, HS], BF16)
            for ch in range(NCH):
                h2T = [(h2_ap[:, kc, :], [R34.tile(h2_ap[:, kc, g * 512:(g + 1) * 512]) for g in range(HS // 512)])
                       for kc in range(KC)]
                aT = [[R34.tile(aT_ap[:, fc, g * 512:(g + 1) * 512]) for g in range(HS // 512)] for fc in range(FC)]
                tb0 = ch * (HS // 128)

                def load_gu(gp):
                    t_ = WG[gp % 2]
                    load_w(t_, t_.ap[:, :, 0, :], wgu_v[:, :, gp * 256:(gp + 1) * 256])
                    load_w(t_, t_.ap[:, :, 1, :], wgu_v[:, :, DFF + gp * 256: DFF + (gp + 1) * 256])

                def load_dn(cp):
                    t_ = WDN[cp % 2]
                    load_w(t_, t_.ap[:, 0:11, :], wdn_v[:, 0:11, cp * 256:(cp + 1) * 256])
                    load_w(t_, t_.ap[:, 11:22, :], wdn_v[:, 11:22, cp * 256:(cp + 1) * 256])

                load_gu(0)
                for g in range(HS // 512):
                    srcs = [(X1[tb0 + g * 4 + i], x1_ap(tb0 + g * 4 + i)) for i in range(4)]
                    norm_transpose(srcs, XN, h2T, G2s, S2s, b, tgc[0], g * 512)
                    tgc[0] += 1
                gidx = 0
                for gp in range(FC // 2):
                    if gp + 1 < FC // 2:
                        load_gu(gp + 1)
                    elif True:
                        load_dn(0)
                    wg = WG[gp % 2]
                    for f2 in range(2):
                        fc = gp * 2 + f2
                        for g in range(HS // 512):
                            ts = slice(g * 512, (g + 1) * 512)
                            bks = [(gidx % 4) * 2, (gidx % 4) * 2 + 1]
                            for u_ in range(2):
                                mmgroup(PS[:, bks[u_], :],
                                        [(wg.ap[:, kc, u_, f2 * 128:(f2 + 1) * 128], h2T[kc][0][:, ts])
                                         for kc in range(KC)], BK[bks[u_]],
                                        [h2T[kc][1][g] for kc in range(KC)] + [wg])
                            sg = SGB_[gidx % 2]
                            S.op(ACT, lambda: nc.scalar.activation(out=sg.ap, in_=PS[:, bks[0], :], func=AF.Silu),
                                 reads=[BK[bks[0]]], writes=[sg])
                            S.op(DVE, lambda: nc.vector.tensor_tensor(out=aT_ap[:, fc, ts], in0=PS[:, bks[1], :],
                                                                      in1=sg.ap, op=ALU.mult),
                                 reads=[BK[bks[1]], sg], writes=[aT[fc][g]])
                            gidx += 1
                for cp in range(4):
                    if cp + 1 < 4:
                        load_dn(cp + 1)
                    wd = WDN[cp % 2]
                    cs = slice(cp * 256, (cp + 1) * 256)
                    for tbl in range(HS // 128):
                        tb = tb0 + tbl
                        b_ = nextbank()
                        mmgroup(PS[:, b_, 0:256], [(aT_ap[:, fc, tbl * 128:(tbl + 1) * 128], wd.ap[:, fc, :])
                                                   for fc in range(FC)], BK[b_],
                                [aT[fc][tbl // 4] for fc in range(FC)] + [wd])
                        tmp = TMP[eidx % 2]
                        S.op(DVE, lambda: nc.vector.tensor_tensor(out=tmp.ap[:, 0:256], in0=PS[:, b_, 0:256],
                                                                  in1=G2BC.ap[:, cs], op=ALU.mult),
                             reads=[BK[b_], G2BC], writes=[tmp])
                        S.op(POOL, lambda: nc.gpsimd.tensor_tensor(out=x1_ap(tb)[:, cs], in0=tmp.ap[:, 0:256],
                                                                   in1=x1_ap(tb)[:, cs], op=ALU.add),
                             reads=[tmp, X1[tb]], writes=[X1[tb]])
                        eidx += 1
                        if cp == 3:
                            S.dma(SP, out_d[b, tb * 128:(tb + 1) * 128, :], x1_ap(tb), out_ds[tb], reads=[X1[tb]])
                R34.reset()

        for ds in out_ds + [dbg_ds]:
            if ds.val:
                SP.eng.wait_ge(ds.sem, ds.val)
    return nc


_CACHE = {}


def _layout_inputs(inp, NB):
    f = lambda a: np.ascontiguousarray(np.asarray(a, dtype=np.float32))
    x = f(inp["x"])
    c = f(inp["c"])
    shared = {
        "w_ada": f(inp["w_ada"][0]),
        "b_adaT": f(inp["b_ada"][0].reshape(48, 128).T),
        "b_ada_row": f(inp["b_ada"][0].reshape(1, 6 * D)),
        "n1gT": f(inp["norm1_g"][0].reshape(KC, 128).T),
        "n2gT": f(inp["norm2_g"][0].reshape(KC, 128).T),
        "w_in": f(inp["w_in"][0]),
        "cwT": f(np.asarray(inp["conv_w"][0]).reshape(3, KC, 128).transpose(2, 0, 1)),
        "gq": f(np.tile(np.asarray(inp["q_norm_g"][0]), 2).reshape(128, 1)),
        "gk": f(np.tile(np.asarray(inp["k_norm_g"][0]), 2).reshape(128, 1)),
        "lamv": f(np.concatenate([np.asarray(inp[k_][0]) for k_ in
                                  ("lambda_q1", "lambda_k1", "lambda_q2", "lambda_k2")]).reshape(1, 256)),
        "subg": f(np.asarray(inp["subln_g"][0]).reshape(128, 1)),
        "w_a_out": f(inp["w_a_out"][0]),
        "w_b_out": f(inp["w_b_out"][0]),
        "w_o": f(inp["w_o"][0]),
        "w_gu": f(inp["w_gu"][0]),
        "w_down": f(inp["w_down"][0]),
    }
    n_cores = x.shape[0] // NB
    maps = []
    for i in range(n_cores):
        m = dict(shared)
        m["x"] = np.ascontiguousarray(x[i * NB:(i + 1) * NB])
        cs = c[i * NB:(i + 1) * NB]
        m["cT"] = np.ascontiguousarray(cs.reshape(NB, KC, 128).transpose(2, 1, 0))
        maps.append(m)
    return maps


def kernel(**inputs):
    x = np.asarray(inputs["x"])
    B, SQ, _ = x.shape
    NB = B // N_CORES
    key = (NB, SQ)
    if key not in _CACHE:
        _CACHE[key] = build(NB, SQ)
    nc = _CACHE[key]
    maps = _layout_inputs(inputs, NB)
    res = run_bass_kernel_spmd(nc, maps, core_ids=list(range(N_CORES)))
    out = np.concatenate([np.asarray(r["out"]) for r in res.results], axis=0)
    return out.astype(np.float32, copy=False)
```

```python
import contextlib
import numpy as np
import concourse.bass as bass
import concourse.mybir as mybir
from concourse.bass_utils import run_bass_kernel_spmd

F32 = mybir.dt.float32
BF16 = mybir.dt.bfloat16
AF = mybir.ActivationFunctionType
ALU = mybir.AluOpType
AX = mybir.AxisListType

D = 1024
KC = 8
H = 8
DFF = 2816
FC = 22
EPS = 1e-6
LAMBDA_INIT = 0.2
N_CORES = 8
INTERLEAVE_PROJ = True
POOL_OFFLOAD = False


class T:
    __slots__ = ("ap", "w", "r", "dsem")

    def __init__(self, ap, deps=None):
        self.ap = ap
        self.w = dict(deps) if deps else {}
        self.r = {}
        self.dsem = None


class DSem:
    def __init__(self, sem):
        self.sem = sem
        self.val = 0
        self.key = "d%d" % id(self)


class Eng:
    def __init__(self, name, eng, sem, self_sync):
        self.name = name
        self.eng = eng
        self.sem = sem
        self.count = 0
        self.waited = {}
        self.self_sync = self_sync


class Sched:
    def __init__(self, nc, es):
        self.nc = nc
        self.es = es
        self.nsem = 0

        def mk(name, eng, ss):
            return Eng(name, eng, self.newsem("e_" + name), ss)

        self.pe = mk("pe", nc.tensor, False)
        self.act = mk("act", nc.scalar, True)
        self.dve = mk("dve", nc.vector, True)
        self.pool = mk("pool", nc.gpsimd, True)
        self.sp = mk("sp", nc.sync, True)

    def newsem(self, name):
        self.nsem += 1
        return self.es.enter_context(self.nc.semaphore(name))

    def dsem(self, name):
        return DSem(self.newsem(name))

    @staticmethod
    def _merge(d, src):
        for k, v in src.items():
            o = d.get(k)
            if o is None or o[1] < v[1]:
                d[k] = v

    def _wait(self, E, deps):
        for k, (sem, val) in deps.items():
            if k == E.name and not E.self_sync:
                continue
            if E.waited.get(k, 0) >= val:
                continue
            E.eng.wait_ge(sem, val)
            E.waited[k] = val

    def op(self, E, fn, reads=(), writes=(), signal=True):
        deps = {}
        for t in reads:
            self._merge(deps, t.w)
        for t in writes:
            self._merge(deps, t.w)
            self._merge(deps, t.r)
        self._wait(E, deps)
        ins = fn()
        if signal:
            E.count += 1
            ins.then_inc(E.sem, 1)
            tk = (E.sem, E.count)
        else:
            tk = (E.sem, E.count + 1)
        k = E.name
        for t in reads:
            o = t.r.get(k)
            if o is None or o[1] < tk[1]:
                t.r[k] = tk
        for t in writes:
            t.w = {k: tk}
            t.r = {}
        return ins

    def dma(self, Q, out_ap, in_ap, ds, reads=(), writes=()):
        deps = {}
        for t in reads:
            self._merge(deps, t.w)
        for t in writes:
            self._merge(deps, t.w)
            self._merge(deps, t.r)
        self._wait(Q, deps)
        ins = Q.eng.dma_start(out=out_ap, in_=in_ap)
        ds.val += 16
        ins.then_inc(ds.sem, 16)
        tk = (ds.sem, ds.val)
        k = ds.key
        for t in reads:
            o = t.r.get(k)
            if o is None or o[1] < tk[1]:
                t.r[k] = tk
        for t in writes:
            t.w = {k: tk}
            t.r = {}
        return ins

    def retire(self, tiles):
        deps = {}
        for t in tiles:
            self._merge(deps, t.w)
            self._merge(deps, t.r)
        return deps


class Region:
    def __init__(self, S_, nc, es, name, nbytes):
        self.S = S_
        self.nbytes = nbytes
        self.t = es.enter_context(nc.sbuf_tensor(name, [128, nbytes // 2], BF16))
        self.tiles = []
        self.fence = {}

    def view(self, off, shape, dt):
        n = int(np.prod(shape[1:]))
        nb = n * (4 if dt == F32 else 2)
        assert off % 4 == 0 and off + nb <= self.nbytes, (off, nb, self.nbytes)
        ap = self.t[:, off // 2:(off + nb) // 2]
        if dt == F32:
            ap = ap.bitcast(F32)
        if len(shape) == 3:
            ap = ap.rearrange("p (a b) -> p a b", b=shape[2])
        elif len(shape) == 4:
            ap = ap.rearrange("p (a b c) -> p a b c", b=shape[2], c=shape[3])
        return ap

    def tile(self, ap):
        t = T(ap, self.fence)
        self.tiles.append(t)
        return t

    def reset(self):
        f = self.S.retire(self.tiles)
        self.S._merge(f, self.fence)
        self.fence = f
        self.tiles = []


def build(NB=4, SQ=2048, debug=False):
    TB = SQ // 128
    TG = SQ // 512
    HS = max(512, SQ // 2)
    nc = bass.Bass("TRN2", target_bir_lowering=False)

    def din(n, sh):
        return nc.dram_tensor(n, sh, F32, kind="ExternalInput").ap()

    x_d = din("x", [NB, SQ, D])
    cT_d = din("cT", [128, KC, NB])
    wada_d = din("w_ada", [D, 6 * D])
    badaT_d = din("b_adaT", [128, 48])
    badar_d = din("b_ada_row", [1, 6 * D])
    n1g_d = din("n1gT", [128, KC])
    n2g_d = din("n2gT", [128, KC])
    win_d = din("w_in", [D, 8 * D])
    cw_d = din("cwT", [128, 3, KC])
    gq_d = din("gq", [128, 1])
    gk_d = din("gk", [128, 1])
    lamv_d = din("lamv", [1, 256])
    subg_d = din("subg", [128, 1])
    wa_d = din("w_a_out", [D, D])
    wb_d = din("w_b_out", [D, D])
    wo_d = din("w_o", [D, D])
    wgu_d = din("w_gu", [D, 2 * DFF])
    wdn_d = din("w_down", [DFF, D])
    out_d = nc.dram_tensor("out", [NB, SQ, D], F32, kind="ExternalOutput").ap()
    gbc_d = nc.dram_tensor("gbc", [NB, 2, 128, D], F32, kind="Internal").ap()

    kv = "(kc p) n -> p kc n"
    wada_v = wada_d.rearrange(kv, p=128)
    win_v = win_d.rearrange(kv, p=128)
    wa_v = wa_d.rearrange(kv, p=128)
    wb_v = wb_d.rearrange(kv, p=128)
    wo_v = wo_d.rearrange(kv, p=128)
    wgu_v = wgu_d.rearrange(kv, p=128)
    wdn_v = wdn_d.rearrange(kv, p=128)

    with contextlib.ExitStack() as es:
        S = Sched(nc, es)
        PE, ACT, DVE, POOL, SP = S.pe, S.act, S.dve, S.pool, S.sp

        def const(name, shape, dt):
            return T(es.enter_context(nc.sbuf_tensor("c_" + name, shape, dt))[:])

        R1 = Region(S, nc, es, "R1", 16 * SQ)
        R2 = Region(S, nc, es, "R2", 16 * SQ)
        R34_MAIN = max(32 * SQ, TB * 2080 + 64 + 28672 + 512)
        R34 = Region(S, nc, es, "R34", R34_MAIN + 16384)
        M = Region(S, nc, es, "M", 56 * 1024)
        ONESF = M.tile(M.view(49408, [128, 128], F32))
        ZEROF = M.tile(M.view(49920, [128, 128], F32))
        LAMV = M.tile(M.view(50432, [128, 256], F32))
        LTMP = M.tile(M.view(51456, [128, 128], F32))
        PS = es.enter_context(nc.psum_tensor("PS", [128, 8, 512], F32))
        BK = [T(PS[:, i, :]) for i in range(8)]

        def bank_bf(i):
            return PS[:, i, :].bitcast(BF16)

        IDENT = const("ident", [128, 128], BF16)
        TRI = const("tri", [128, 128], BF16)
        BLK = const("blk", [128, 128], BF16)
        MHALF = const("mhalf", [128, 8], F32)
        G1s = const("G1s", [128, KC, NB], F32)
        S1s = const("S1s", [128, KC, NB], F32)
        G2s = const("G2s", [128, KC, NB], F32)
        S2s = const("S2s", [128, KC, NB], F32)
        N1G = const("n1g", [128, KC], F32)
        N2G = const("n2g", [128, KC], F32)
        BADAT = const("badaT", [128, 48], F32)
        CW = const("cw", [128, 3, KC], F32)
        GQ = const("gq", [128, 1], F32)
        GK = const("gk", [128, 1], F32)
        SUBG = const("subg", [128, 1], F32)
        NEGLAM = const("neglam", [128, 1], F32)
        CT = const("ct", [128, KC, NB], F32)
        SCT = const("sct", [128, KC, NB], BF16)
        LS = const("ls", [128, 2], F32)
        SSQ = [const("ssq%d" % i, [128, 1], F32) for i in range(4)]
        EPSC = const("epsc", [128, 1], F32)
        JUNK = es.enter_context(nc.sbuf_tensor("junk", [128, 1024], BF16))

        dsetup = [S.dsem("dsu%d" % i) for i in range(10)]
        k = 0
        for t_, src in ((CT, cT_d), (BADAT, badaT_d), (N1G, n1g_d), (N2G, n2g_d), (CW, cw_d), (GQ, gq_d),
                        (GK, gk_d), (SUBG, subg_d)):
            S.dma(SP, t_.ap, src, dsetup[k], writes=[t_])
            k += 1
        S.dma(SP, LAMV.ap, lamv_d.partition_broadcast(128), dsetup[k], writes=[LAMV])

        S.op(POOL, lambda: nc.gpsimd.memset(ONESF.ap, 1.0), writes=[ONESF])
        S.op(POOL, lambda: nc.gpsimd.memset(MHALF.ap, -0.5), writes=[MHALF])
        S.op(POOL, lambda: nc.gpsimd.memset(EPSC.ap, EPS), writes=[EPSC])
        S.op(POOL, lambda: nc.gpsimd.affine_select(out=IDENT.ap, in_=ONESF.ap, pattern=[[-1, 128]],
                                                   compare_op=ALU.is_equal, fill=0.0, base=0,
                                                   channel_multiplier=1), reads=[ONESF], writes=[IDENT])
        S.op(POOL, lambda: nc.gpsimd.memset(ZEROF.ap, 0.0), writes=[ZEROF])
        S.op(POOL, lambda: nc.gpsimd.affine_select(out=TRI.ap, in_=ZEROF.ap, pattern=[[1, 128]],
                                                   compare_op=ALU.is_ge, fill=-30000.0, base=0,
                                                   channel_multiplier=-1), reads=[ZEROF], writes=[TRI])
        S.op(POOL, lambda: nc.gpsimd.memset(BLK.ap, 0.0), writes=[BLK])
        S.op(POOL, lambda: nc.gpsimd.memset(BLK.ap[0:64, 0:64], 1.0), writes=[BLK])
        S.op(POOL, lambda: nc.gpsimd.memset(BLK.ap[64:128, 64:128], 1.0), writes=[BLK])
        S.op(DVE, lambda: nc.vector.tensor_scalar(out=SUBG.ap, in0=SUBG.ap, scalar1=1.0 - LAMBDA_INIT,
                                                  scalar2=None, op0=ALU.mult), reads=[SUBG], writes=[SUBG])
        S.op(DVE, lambda: nc.vector.tensor_tensor(out=LTMP.ap[:, 0:64], in0=LAMV.ap[:, 0:64],
                                                  in1=LAMV.ap[:, 64:128], op=ALU.mult), reads=[LAMV], writes=[LTMP])
        S.op(DVE, lambda: nc.vector.tensor_tensor(out=LTMP.ap[:, 64:128], in0=LAMV.ap[:, 128:192],
                                                  in1=LAMV.ap[:, 192:256], op=ALU.mult), reads=[LAMV], writes=[LTMP])
        S.op(DVE, lambda: nc.vector.reduce_sum(out=LS.ap, in_=LTMP.ap.rearrange("p (a b) -> p a b", b=64),
                                               axis=AX.X), reads=[LTMP], writes=[LS])
        S.op(ACT, lambda: nc.scalar.activation(out=LS.ap, in_=LS.ap, func=AF.Exp), reads=[LS], writes=[LS])
        S.op(DVE, lambda: nc.vector.tensor_tensor(out=NEGLAM.ap, in0=LS.ap[:, 1:2], in1=LS.ap[:, 0:1],
                                                  op=ALU.subtract), reads=[LS], writes=[NEGLAM])
        S.op(DVE, lambda: nc.vector.tensor_scalar(out=NEGLAM.ap, in0=NEGLAM.ap, scalar1=-LAMBDA_INIT,
                                                  scalar2=None, op0=ALU.add), reads=[NEGLAM], writes=[NEGLAM])
        S.op(ACT, lambda: nc.scalar.activation(out=SCT.ap, in_=CT.ap, func=AF.Silu), reads=[CT], writes=[SCT])

        WADA = [M.tile(M.view(i * 16384, [128, KC, 1024], BF16)) for i in range(2)]
        for i_, t_ in enumerate(WADA):
            t_.dsem = S.dsem("dwada%d" % i_)
        BBC = M.tile(M.view(32768, [128, 1024], F32))
        BBC.dsem = S.dsem("dbbc")
        GST = [M.tile(M.view(36864 + i * 4096, [128, 1024], F32)) for i in range(2)]
        SCB = [M.tile(M.view(45056 + i * 2048, [128, KC, 128], BF16)) for i in range(2)]
        MODT = M.tile(M.view(49152, [128, KC, NB], F32))
        gst_ds = [S.dsem("dgst%d" % i) for i in range(2)]
        GBC_D = [[T(gbc_d[b, w]) for w in range(2)] for b in range(NB)]

        def load_wada(j, pos):
            t_ = WADA[pos % 2]
            for hf in range(2):
                S.dma(POOL, t_.ap[:, :, hf * 512:(hf + 1) * 512],
                      wada_v[:, :, j * 1024 + hf * 512: j * 1024 + (hf + 1) * 512], t_.dsem, writes=[t_])
            return t_

        nbk = [0]

        def nextbank():
            b = nbk[0] % 8
            nbk[0] += 1
            return b

        order = [0, 1, 3, 4, 2, 5]
        cur = load_wada(order[0], 0)
        gcount = 0
        for oi, j in enumerate(order):
            nxt = load_wada(order[oi + 1], oi + 1) if oi + 1 < len(order) else None
            if j in (0, 1, 3, 4):
                b_ = nextbank()
                for fcn in range(KC):
                    for kc in range(KC):
                        S.op(PE, lambda: nc.tensor.matmul(PS[:, b_, fcn * NB:(fcn + 1) * NB],
                                                          lhsT=cur.ap[:, kc, fcn * 128:(fcn + 1) * 128],
                                                          rhs=SCT.ap[:, kc, :], start=(kc == 0), stop=(kc == KC - 1),
                                                          skip_group_check=True),
                             reads=[cur, SCT], writes=[BK[b_]], signal=(kc == KC - 1))
                psv = PS[:, b_, 0:KC * NB].rearrange("p (a b) -> p a b", b=NB)
                bias = BADAT.ap[:, j * 8:(j + 1) * 8].unsqueeze(2).to_broadcast([128, KC, NB])
                dst = {0: S1s, 1: MODT, 3: S2s, 4: MODT}[j]
                S.op(DVE, lambda: nc.vector.tensor_tensor(out=dst.ap, in0=psv, in1=bias, op=ALU.add),
                     reads=[BK[b_], BADAT], writes=[dst])
                if j in (1, 4):
                    gt, ng = (G1s, N1G) if j == 1 else (G2s, N2G)
                    S.op(DVE, lambda: nc.vector.scalar_tensor_tensor(
                        out=gt.ap, in0=MODT.ap, scalar=1.0,
                        in1=ng.ap.unsqueeze(2).to_broadcast([128, KC, NB]), op0=ALU.add, op1=ALU.mult),
                        reads=[MODT, ng], writes=[gt])
            else:
                w_ = 0 if j == 2 else 1
                S.dma(SP, BBC.ap, badar_d[:, j * 1024:(j + 1) * 1024].partition_broadcast(128), BBC.dsem,
                      writes=[BBC])
                for b in range(NB):
                    scb = SCB[gcount % 2]
                    gst = GST[gcount % 2]
                    S.op(DVE, lambda: nc.vector.tensor_copy(
                        out=scb.ap, in_=SCT.ap[:, :, b:b + 1].to_broadcast([128, KC, 128])),
                        reads=[SCT], writes=[scb])
                    for hf in range(2):
                        b_ = nextbank()
                        for kc in range(KC):
                            S.op(PE, lambda: nc.tensor.matmul(PS[:, b_, :], lhsT=scb.ap[:, kc, :],
                                                              rhs=cur.ap[:, kc, hf * 512:(hf + 1) * 512],
                                                              start=(kc == 0), stop=(kc == KC - 1)),
                                 reads=[scb, cur], writes=[BK[b_]], signal=(kc == KC - 1))
                        S.op(DVE, lambda: nc.vector.tensor_tensor(out=gst.ap[:, hf * 512:(hf + 1) * 512],
                                                                  in0=PS[:, b_, :],
                                                                  in1=BBC.ap[:, hf * 512:(hf + 1) * 512],
                                                                  op=ALU.add),
                             reads=[BK[b_], BBC], writes=[gst])
                    S.dma(SP, gbc_d[b, w_], gst.ap, gst_ds[gcount % 2], reads=[gst], writes=[GBC_D[b][w_]])
                    gcount += 1
            cur = nxt
        M.reset()

        out_ds = [S.dsem("dout%d" % i) for i in range(TB)]
        dbg_ds = S.dsem("ddbg")

        def dump(name, ap, tiles, dt=BF16):
            if not debug:
                return
            d_ = nc.dram_tensor("dbg_" + name, list(ap.shape), dt, kind="ExternalOutput").ap()
            S.dma(SP, d_, ap, dbg_ds, reads=tiles)

        for nm, t_ in (("G1s", G1s), ("S1s", S1s), ("G2s", G2s), ("S2s", S2s), ("NEGLAM", NEGLAM), ("SUBG", SUBG)):
            dump(nm, t_.ap, [t_], F32)
        xin_ds = [S.dsem("dxin%d" % i) for i in range(8)]
        wv_ds = [S.dsem("dwv%d" % i) for i in range(2)]
        wq_ds = [S.dsem("dwq%d" % i) for i in range(4)]
        wc_ds = [S.dsem("dwc%d" % i) for i in range(6)]
        wd_ds = [S.dsem("dwd%d" % i) for i in range(8)]
        wo_ds = S.dsem("dwo")
        wg_ds = [S.dsem("dwg%d" % i) for i in range(2)]
        wdn_ds = [S.dsem("dwdn%d" % i) for i in range(2)]
        g_ds = [S.dsem("dg%d" % i) for i in range(2)]

        def evac_mod(idx, out_ap, in_ap, g_ap, s_ap, reads, writes):
            if idx % 2 == 0:
                S.op(ACT, lambda: nc.scalar.activation(out=out_ap, in_=in_ap, func=AF.Identity, scale=g_ap,
                                                       bias=s_ap), reads=reads, writes=writes)
            else:
                S.op(DVE, lambda: nc.vector.tensor_scalar(out=out_ap, in0=in_ap, scalar1=g_ap, scalar2=s_ap,
                                                          op0=ALU.mult, op1=ALU.add), reads=reads, writes=writes)

        def rstd_from_ssq(ssq, n):
            S.op(DVE, lambda: nc.vector.tensor_scalar(out=ssq.ap, in0=ssq.ap, scalar1=1.0 / n, scalar2=EPS,
                                                      op0=ALU.mult, op1=ALU.add), reads=[ssq], writes=[ssq])
            S.op(POOL, lambda: nc.gpsimd.tensor_tensor(out=ssq.ap, in0=ssq.ap, in1=MHALF.ap[:, 0:1], op=ALU.pow),
                 reads=[ssq, MHALF], writes=[ssq])

        def norm_tile(src_t, src_ap, ssq, xn):
            S.op(ACT, lambda: nc.scalar.activation(out=JUNK[:, :], in_=src_ap, func=AF.Square,
                                                   accum_out=ssq.ap), reads=[src_t], writes=[ssq])
            rstd_from_ssq(ssq, D)
            S.op(DVE, lambda: nc.vector.tensor_scalar(out=xn.ap, in0=src_ap, scalar1=ssq.ap, scalar2=None,
                                                      op0=ALU.mult), reads=[src_t, ssq], writes=[xn])
            return xn

        def transpose_group(xns, dstT, Gs, Ss, b, col0, bank_base):
            for kc in range(KC):
                b_ = bank_base + kc // 2
                pv = bank_bf(b_)
                c0 = (kc % 2) * 512
                for i in range(4):
                    S.op(PE, lambda: nc.tensor.transpose(pv[:, c0 + i * 128: c0 + (i + 1) * 128],
                                                         xns[i].ap[:, kc * 128:(kc + 1) * 128], IDENT.ap),
                         reads=[xns[i], IDENT], writes=[BK[b_]], signal=(i == 3))
                evac_mod(kc, dstT[kc][0][:, col0:col0 + 512], pv[:, c0:c0 + 512], Gs.ap[:, kc, b:b + 1],
                         Ss.ap[:, kc, b:b + 1], [BK[b_], Gs, Ss], [dstT[kc][1][col0 // 512]])

        def norm_transpose(src_tiles, XN, dstT, Gs, Ss, b, tg_global, col0):
            xns = [norm_tile(src_t, src_ap, SSQ[i], XN[(tg_global * 4 + i) % len(XN)])
                   for i, (src_t, src_ap) in enumerate(src_tiles)]
            transpose_group(xns, dstT, Gs, Ss, b, col0, (tg_global % 2) * 4)

        def load_w(t_, dst_ap, src_ap):
            S.dma(POOL, dst_ap, src_ap, t_.dsem, writes=[t_])

        def mmgroup(out_ap, pairs, bk, reads, **kw):
            n = len(pairs)
            for i, (l, r) in enumerate(pairs):
                S.op(PE, lambda: nc.tensor.matmul(out_ap, lhsT=l, rhs=r, start=(i == 0), stop=(i == n - 1), **kw),
                     reads=reads, writes=[bk], signal=(i == n - 1))

        tgc = [0]
        ecount = [0]
        WV = [T(R34.view(R34_MAIN + i * 8192, [128, KC, 512], BF16)) for i in range(2)]
        for i in range(2):
            WV[i].dsem = wv_ds[i]
            load_w(WV[i], WV[i].ap, win_v[:, :, 2048 + i * 512: 2048 + (i + 1) * 512])

        for b in range(NB):
            R1.reset()
            M.reset()
            hT_ap = R1.view(0, [128, KC, SQ], BF16)
            hT = [(hT_ap[:, kc, :], [R1.tile(hT_ap[:, kc, g * 512:(g + 1) * 512]) for g in range(TG)])
                  for kc in range(KC)]
            XIN = [M.tile(M.view(i * 4096, [128, 1024], F32)) for i in range(8)]
            for i in range(8):
                XIN[i].dsem = xin_ds[i]
            XN = [M.tile(M.view(32768 + i * 2048, [128, 1024], BF16)) for i in range(4)]
            R34.reset()
            R2.reset()
            VA_ap = R34.view(0, [128, TB, H, 130], BF16)
            VA = [[R34.tile(VA_ap[:, tb, hf * 4:(hf + 1) * 4, :]) for hf in range(2)] for tb in range(TB)]
            allva = [VA[tb][hf] for tb in range(TB) for hf in range(2)]
            S.op(POOL, lambda: nc.gpsimd.memset(VA_ap[:, :, :, 128:130], 1.0), writes=allva)

            vcnt = [0]

            def v_proj(tb):
                for hf in range(2):
                    b_ = 4 + vcnt[0] % 4
                    vcnt[0] += 1
                    mmgroup(PS[:, b_, :], [(hT[kc][0][:, tb * 128:(tb + 1) * 128], WV[hf].ap[:, kc, :])
                                           for kc in range(KC)], BK[b_],
                            [hT[kc][1][tb // 4] for kc in range(KC)] + [WV[hf]])
                    src = PS[:, b_, :].rearrange("p (a b) -> p a b", b=128)
                    dst = VA_ap[:, tb, hf * 4:(hf + 1) * 4, 0:128]
                    if hf == 0:
                        S.op(ACT, lambda: nc.scalar.copy(out=dst, in_=src), reads=[BK[b_]], writes=[VA[tb][hf]])
                    else:
                        S.op(DVE, lambda: nc.vector.tensor_copy(out=dst, in_=src), reads=[BK[b_]],
                             writes=[VA[tb][hf]])

            def load_x(tg):
                for i in range(4):
                    tb = tg * 4 + i
                    xin = XIN[(tg % 2) * 4 + i]
                    S.dma(SP, xin.ap, x_d[b, tb * 128:(tb + 1) * 128, :], xin.dsem, writes=[xin])

            load_x(0)
            for tg in range(TG):
                if tg + 1 < TG:
                    load_x(tg + 1)
                xns = []
                for i in range(4):
                    xin = XIN[(tg % 2) * 4 + i]
                    xns.append(norm_tile(xin, xin.ap, SSQ[i], XN[i]))
                    if tg >= 1:
                        v_proj((tg - 1) * 4 + i)
                transpose_group(xns, hT, G1s, S1s, b, tg * 512, 0)
            for i in range(4):
                v_proj((TG - 1) * 4 + i)

            oT_ap = R2.view(0, [128, H, SQ], BF16)
            oT = [[R2.tile(oT_ap[:, h, g * 512:(g + 1) * 512]) for g in range(TG)] for h in range(H)]
            M.reset()
            WQ = [M.tile(M.view(i * 4096, [128, KC, 256], BF16)) for i in range(2)]
            WK = [M.tile(M.view(8192 + i * 4096, [128, KC, 256], BF16)) for i in range(2)]
            for i in range(2):
                WQ[i].dsem = wq_ds[i]
                WK[i].dsem = wq_ds[2 + i]
            QT = [M.view(16384 + i * 2 * SQ, [128, SQ], BF16) for i in range(2)]
            KT = [M.view(16384 + 4 * SQ + i * 2 * SQ, [128, SQ], BF16) for i in range(2)]
            QTt = [[M.tile(QT[i][:, g * 512:(g + 1) * 512]) for g in range(TG)] for i in range(2)]
            KTt = [[M.tile(KT[i][:, g * 512:(g + 1) * 512]) for g in range(TG)] for i in range(2)]
            o = 16384 + 8 * SQ
            ET = [M.tile(M.view(o + i * 2048, [128, 2, 512], BF16)) for i in range(3)]
            o += 6144
            o34 = (TB * 2080 + 63) // 64 * 64
            NQS = 8
            QS = [R34.tile(R34.view(o34 + i * 2048, [128, 512], F32)) for i in range(NQS)]
            o34 += 2048 * NQS
            SQB = [R34.tile(R34.view(o34 + i * 1024, [128, 512], BF16)) for i in range(NQS)]
            o34 += 1024 * NQS
            SDB = [R34.tile(R34.view(o34 + i * 2048, [128, 512], F32)) for i in range(2)]
            o34 += 4096
            RC = [M.tile(M.view(o + i * 32, [128, 8], F32)) for i in range(2)]
            o += 64
            SS4 = [M.tile(M.view(o + i * 32, [128, 4], F32)) for i in range(2)]
            o += 64
            AS = [M.tile(M.view(o + i * 4128, [128, 8, 129], F32)) for i in range(2)]
            o += 8256
            ON = [M.tile(M.view(o + i * 1024, [128, 4, 128], BF16)) for i in range(2)]
            o += 2048
            assert o <= M.nbytes, o

            def load_qk(hp):
                sl = hp % 2
                load_w(WQ[sl], WQ[sl].ap, win_v[:, :, hp * 256:(hp + 1) * 256])
                load_w(WK[sl], WK[sl].ap, win_v[:, :, 1024 + hp * 256: 1024 + (hp + 1) * 256])

            load_qk(0)
            nq = [0]
            acc_slots = [((4 + s_, il * 129) if il < 3 else (6, s_ * 129)) for il in range(4) for s_ in range(2)]
            deferred = []
            nchunk = [0]

            def flush_deferred():
                while deferred:
                    deferred.pop(0)()
            tiles_ = [(tc, w) for tc in range(TG) for w in range(2)]

            def make_units(h_):
                hp_, par_ = h_ // 2, h_ % 2

                def unit_a(t, b_, half=None):
                    tc, w = tiles_[t]
                    W_ = (WQ, WK)[w][hp_ % 2]
                    kcs = range(KC) if half is None else range(half * 4, half * 4 + 4)
                    for kc in kcs:
                        S.op(PE, lambda: nc.tensor.matmul(PS[:, b_, :],
                                                          lhsT=W_.ap[:, kc, (h_ % 2) * 128:(h_ % 2 + 1) * 128],
                                                          rhs=hT[kc][0][:, tc * 512:(tc + 1) * 512],
                                                          start=(kc == 0), stop=(kc == KC - 1)),
                             reads=[hT[kc][1][tc], W_], writes=[BK[b_]], signal=(kc == KC - 1))
                    if half == 0:
                        return
                    sq, qs = SQB[t % NQS], QS[t % NQS]
                    S.op(DVE, lambda: nc.vector.tensor_copy(out=qs.ap, in_=PS[:, b_, :]), reads=[BK[b_]],
                         writes=[qs])
                    E_, e_ = (POOL, nc.gpsimd) if POOL_OFFLOAD else (DVE, nc.vector)
                    S.op(E_, lambda: e_.tensor_tensor(out=sq.ap, in0=qs.ap, in1=qs.ap, op=ALU.mult),
                         reads=[qs], writes=[sq])

                def unit_b(t, b2):
                    tc, w = tiles_[t]
                    gvec = (GQ, GK)[w]
                    dst_ap = (QT, KT)[w][par_]
                    dst_t = (QTt, KTt)[w][par_]
                    sq, qs, sd = SQB[t % NQS], QS[t % NQS], SDB[t % 2]
                    S.op(PE, lambda: nc.tensor.matmul(PS[:, b2, :], lhsT=BLK.ap, rhs=sq.ap, start=True,
                                                      stop=True), reads=[BLK, sq], writes=[BK[b2]])
                    S.op(ACT, lambda: nc.scalar.activation(out=sd.ap, in_=PS[:, b2, :], func=AF.Ln,
                                                           scale=1.0 / 64, bias=EPSC.ap),
                         reads=[BK[b2], EPSC], writes=[sd])
                    S.op(ACT, lambda: nc.scalar.activation(out=sd.ap, in_=sd.ap, func=AF.Exp, scale=-0.5),
                         reads=[sd], writes=[sd])
                    S.op(DVE, lambda: nc.vector.scalar_tensor_tensor(
                        out=dst_ap[:, tc * 512:(tc + 1) * 512], in0=qs.ap, scalar=gvec.ap, in1=sd.ap,
                        op0=ALU.mult, op1=ALU.mult), reads=[qs, gvec, sd], writes=[dst_t[tc]])

                return unit_a, unit_b

            def proj_standalone(h_):
                ua0, ub0 = make_units(h_)
                for t in range(len(tiles_)):
                    ua0(t, t % 4)
                    if t >= 2:
                        ub0(t - 2, 4 + t % 2)
                for t in range(max(0, len(tiles_) - 2), len(tiles_)):
                    ub0(t, 4 + t % 2)

            proj_standalone(0)

            for h in range(H):
                hp = h // 2
                if h % 2 == 0 and hp + 1 < H // 2:
                    load_qk(hp + 1)
                par = h % 2
                steps = [(qc, j) for qc in range(TG) for j in range(4 * qc + 4)]
                qt, kt, qtt, ktt = QT[par], KT[par], QTt[par], KTt[par]

                def s_stage(si):
                    qc, j = steps[si]
                    il0 = max(0, j - 4 * qc)
                    ncols = 512 - il0 * 128
                    q0 = qc * 512 + il0 * 128
                    b0 = 2 * (si % 2)
                    et = ET[si % 3]
                    diag = j >= 4 * qc
                    for s_ in range(2):
                        S.op(PE, lambda: nc.tensor.matmul(PS[:, b0 + s_, 0:ncols],
                                                          lhsT=kt[s_ * 64:(s_ + 1) * 64, j * 128:(j + 1) * 128],
                                                          rhs=qt[s_ * 64:(s_ + 1) * 64, q0:q0 + ncols],
                                                          start=True, stop=not diag),
                             reads=[ktt[j // 4], qtt[qc]], writes=[BK[b0 + s_]], signal=not diag)
                    if diag:
                        for s_ in range(2):
                            S.op(PE, lambda: nc.tensor.matmul(PS[:, b0 + s_, 0:128], lhsT=IDENT.ap, rhs=TRI.ap,
                                                              start=False, stop=True),
                                 reads=[IDENT, TRI], writes=[BK[b0 + s_]])
                    S.op(ACT, lambda: nc.scalar.activation(out=et.ap[:, :, 0:ncols], in_=PS[:, b0:b0 + 2, 0:ncols],
                                                           func=AF.Exp, scale=0.125),
                         reads=[BK[b0], BK[b0 + 1]], writes=[et])

                def av_stage(si):
                    qc, j = steps[si]
                    il0 = max(0, j - 4 * qc)
                    et = ET[si % 3]
                    mms = []
                    for il in range(il0, 4):
                        for s_ in range(2):
                            bk_, off = acc_slots[il * 2 + s_]
                            mms.append((bk_, off, il, s_))
                    for n_, (bk_, off, il, s_) in enumerate(mms):
                        c0 = (il - il0) * 128
                        S.op(PE, lambda: nc.tensor.matmul(PS[:, bk_, off:off + 129],
                                                          lhsT=et.ap[:, s_, c0:c0 + 128],
                                                          rhs=VA_ap[:, j, h, 0:129],
                                                          start=(j == 0 and off == 0), stop=(j == 4 * qc + il),
                                                          skip_group_check=True),
                             reads=[et, VA[j][h // 4]], writes=[BK[bk_]], signal=(n_ == len(mms) - 1))
                    if j == 4 * qc + 3:
                        evac_chunk(qc)

                def evac_chunk(qc, h=h):
                    k_ = nchunk[0] % 2
                    nchunk[0] += 1
                    as_, rc, ss, on = AS[k_], RC[k_], SS4[k_], ON[k_]
                    as4 = as_.ap.rearrange("p (s i) c -> p s i c", i=4)
                    S.op(DVE, lambda: nc.vector.tensor_copy(
                        out=as4[:, :, 0:3, :],
                        in_=PS[:, 4:6, 0:387].rearrange("p s (a c) -> p s a c", c=129)),
                        reads=[BK[4], BK[5]], writes=[as_])
                    S.op(DVE, lambda: nc.vector.tensor_copy(
                        out=as4[:, :, 3, :], in_=PS[:, 6, 0:258].rearrange("p (a c) -> p a c", c=129)),
                        reads=[BK[6]], writes=[as_])
                    if boundary_units:
                        boundary_units.pop(0)()
                    flush_deferred()
                    S.op(DVE, lambda: nc.vector.reciprocal(out=rc.ap, in_=as_.ap[:, :, 128]), reads=[as_],
                         writes=[rc])
                    S.op(DVE, lambda: nc.vector.tensor_scalar(out=rc.ap[:, 4:8], in0=rc.ap[:, 4:8],
                                                              scalar1=NEGLAM.ap, scalar2=None, op0=ALU.mult),
                         reads=[rc, NEGLAM], writes=[rc])
                    E_, e_ = (POOL, nc.gpsimd) if POOL_OFFLOAD else (DVE, nc.vector)
                    S.op(E_, lambda: e_.tensor_tensor(
                        out=as_.ap[:, :, 0:128], in0=as_.ap[:, :, 0:128],
                        in1=rc.ap.unsqueeze(2).to_broadcast([128, 8, 128]), op=ALU.mult),
                        reads=[as_, rc], writes=[as_])
                    S.op(E_, lambda: e_.tensor_tensor(out=as_.ap[:, 0:4, 0:128], in0=as_.ap[:, 0:4, 0:128],
                                                      in1=as_.ap[:, 4:8, 0:128], op=ALU.add),
                         reads=[as_], writes=[as_])
                    S.op(DVE, lambda: nc.vector.tensor_tensor(out=as_.ap[:, 4:8, 0:128], in0=as_.ap[:, 0:4, 0:128],
                                                              in1=as_.ap[:, 0:4, 0:128], op=ALU.mult),
                         reads=[as_], writes=[as_])
                    S.op(DVE, lambda: nc.vector.reduce_sum(out=ss.ap, in_=as_.ap[:, 4:8, 0:128], axis=AX.X),
                         reads=[as_], writes=[ss])
                    S.op(DVE, lambda: nc.vector.tensor_scalar(out=ss.ap, in0=ss.ap, scalar1=1.0 / 128, scalar2=EPS,
                                                              op0=ALU.mult, op1=ALU.add), reads=[ss], writes=[ss])
                    S.op(POOL, lambda: nc.gpsimd.tensor_tensor(out=ss.ap, in0=ss.ap, in1=MHALF.ap[:, 0:4],
                                                               op=ALU.pow), reads=[ss, MHALF], writes=[ss])
                    S.op(POOL, lambda: nc.gpsimd.tensor_tensor(
                        out=on.ap, in0=as_.ap[:, 0:4, 0:128], in1=ss.ap.unsqueeze(2).to_broadcast([128, 4, 128]),
                        op=ALU.mult), reads=[as_, ss], writes=[on])

                    def fin():
                        for il in range(4):
                            S.op(PE, lambda: nc.tensor.transpose(bank_bf(7)[:, il * 128:(il + 1) * 128],
                                                                 on.ap[:, il, :], IDENT.ap),
                                 reads=[on, IDENT], writes=[BK[7]], signal=(il == 3))
                        S.op(DVE, lambda: nc.vector.tensor_scalar(out=oT_ap[:, h, qc * 512:(qc + 1) * 512],
                                                                  in0=bank_bf(7)[:, 0:512], scalar1=SUBG.ap,
                                                                  scalar2=None, op0=ALU.mult),
                             reads=[BK[7], SUBG], writes=[oT[h][qc]])
                    deferred.append(fin)

                sched_ = {}
                boundary_units = []
                if h + 1 < H and INTERLEAVE_PROJ:
                    ua, ub = make_units(h + 1)
                    sp_ = max(2, len(steps) // len(tiles_))
                    nb_ = min(TG, len(tiles_))
                    for t in range(nb_):
                        boundary_units.append(lambda t=t, ua=ua: ua(t, 7))
                    rest = list(range(nb_, len(tiles_)))
                    slots_ = []
                    for qc_ in range(1, TG):
                        base = sum(4 * q_ + 4 for q_ in range(qc_))
                        ln_ = 4 * qc_ + 4
                        npos = 1 if qc_ < TG - 1 else max(1, len(rest) - (TG - 2))
                        for p_ in range(npos):
                            slots_.append(base + (p_ + 1) * ln_ // (npos + 1) - 1)
                    for t, st_ in zip(rest, slots_):
                        sched_.setdefault(st_, []).append(lambda t=t, ua=ua: ua(t, 7, 0))
                        sched_.setdefault(st_ + 1, []).append(lambda t=t, ua=ua: ua(t, 7, 1))
                    for t in rest[len(slots_):]:
                        sched_.setdefault(10 ** 6 + t, []).append(lambda t=t, ua=ua: ua(t, 7))
                flush_deferred()
                s_stage(0)
                if len(steps) > 1:
                    s_stage(1)
                for si in range(len(steps)):
                    if si + 2 < len(steps):
                        s_stage(si + 2)
                    av_stage(si)
                    for u_ in sched_.pop(si, []):
                        u_()
                while boundary_units:
                    boundary_units.pop(0)()
                for k_ in sorted(sched_):
                    for u_ in sched_[k_]:
                        u_()
                if h + 1 < H and not INTERLEAVE_PROJ:
                    proj_standalone(h + 1)
                if h + 1 < H and INTERLEAVE_PROJ:
                    for t in range(len(tiles_)):
                        ub(t, 4 + t % 2)

            flush_deferred()
            if b == 0:
                dump("VA", VA_ap, allva)
                dump("QT", QT[1], QTt[1])
                dump("KT", KT[1], KTt[1])
                dump("oT", oT_ap, [t_ for h_ in range(H) for t_ in oT[h_]])
            R34.reset()
            M.reset()
            ya_ap = R34.view(0, [128, KC, SQ], BF16)
            yaT = [[R34.tile(ya_ap[:, f, g * 512:(g + 1) * 512]) for g in range(TG)] for f in range(KC)]
            mT_ap = R34.view(16 * SQ, [128, KC, SQ], BF16)
            mT = [[R34.tile(mT_ap[:, f, g * 512:(g + 1) * 512]) for g in range(TG)] for f in range(KC)]
            WC = [[M.tile(M.view((i * 3 + j) * 4096, [128, KC, 256], BF16)) for j in range(3)] for i in range(2)]
            for i in range(2):
                for j in range(3):
                    WC[i][j].dsem = wc_ds[i * 3 + j]
            o = 24576
            U_ap = M.view(o, [128, 2 + SQ], F32)
            Ut = [M.tile(U_ap[:, 2 + g * 512: 2 + (g + 1) * 512]) for g in range(TG)]
            Upad = M.tile(U_ap[:, 0:2])
            o += (2 + SQ) * 4
            o = (o + 31) // 32 * 32
            CCS = [M.tile(M.view(o + i * 2048, [128, 512], F32)) for i in range(2)]
            o += 4096
            YB = [M.tile(M.view(o + i * 2048, [128, 512], F32)) for i in range(2)]
            o += 4096
            assert o <= M.nbytes, o
            S.op(POOL, lambda: nc.gpsimd.memset(U_ap[:, 0:2], 0.0), writes=[Upad])

            def load_conv(fp):
                sl = fp % 2
                for j, base in enumerate((4096, 5120, 3072)):
                    load_w(WC[sl][j], WC[sl][j].ap, win_v[:, :, base + fp * 256: base + (fp + 1) * 256])

            load_conv(0)
            cidx = 0
            for f in range(KC):
                fp = f // 2
                if f % 2 == 0 and fp + 1 < KC // 2:
                    load_conv(fp + 1)
                Wcc, Wcx, Wcb = WC[fp % 2]
                for tc in range(TG):
                    bks = [(cidx % 2) * 3 + i for i in range(3)]
                    for W_, b_ in zip((Wcc, Wcx, Wcb), bks):
                        mmgroup(PS[:, b_, :], [(W_.ap[:, kc, (f % 2) * 128:(f % 2 + 1) * 128],
                                                hT[kc][0][:, tc * 512:(tc + 1) * 512]) for kc in range(KC)],
                                BK[b_], [hT[kc][1][tc] for kc in range(KC)] + [W_])
                    ccs = CCS[cidx % 2]
                    yb = YB[cidx % 2]
                    S.op(ACT, lambda: nc.scalar.copy(out=ccs.ap, in_=PS[:, bks[0], :]), reads=[BK[bks[0]]],
                         writes=[ccs])
                    ut = Ut[tc]
                    S.op(DVE, lambda: nc.vector.tensor_tensor(out=ut.ap, in0=PS[:, bks[1], :], in1=ccs.ap,
                                                              op=ALU.mult), reads=[BK[bks[1]], ccs], writes=[ut])
                    prev = [Ut[tc - 1]] if tc > 0 else [Upad]
                    c0 = 2 + tc * 512
                    S.op(POOL, lambda: nc.gpsimd.tensor_scalar(out=yb.ap, in0=U_ap[:, c0 - 2:c0 - 2 + 512],
                                                               scalar1=CW.ap[:, 0, f:f + 1], scalar2=0.0,
                                                               op0=ALU.mult, op1=ALU.add),
                         reads=[ut, CW] + prev, writes=[yb])
                    S.op(DVE, lambda: nc.vector.scalar_tensor_tensor(out=yb.ap, in0=U_ap[:, c0 - 1:c0 - 1 + 512],
                                                                      scalar=CW.ap[:, 1, f:f + 1], in1=yb.ap,
                                                                      op0=ALU.mult, op1=ALU.add),
                         reads=[ut, CW, yb] + prev, writes=[yb])
                    S.op(DVE, lambda: nc.vector.scalar_tensor_tensor(out=yb.ap, in0=U_ap[:, c0:c0 + 512],
                                                                      scalar=CW.ap[:, 2, f:f + 1], in1=yb.ap,
                                                                      op0=ALU.mult, op1=ALU.add),
                         reads=[ut, CW, yb], writes=[yb])
                    S.op(DVE, lambda: nc.vector.tensor_tensor(out=ya_ap[:, f, tc * 512:(tc + 1) * 512],
                                                              in0=PS[:, bks[2], :], in1=yb.ap, op=ALU.mult),
                         reads=[BK[bks[2]], yb], writes=[yaT[f][tc]])
                    cidx += 1

            M.reset()
            WD4 = [[M.tile(M.view((i * 4 + j) * 4096, [128, KC, 256], BF16)) for j in range(4)] for i in range(2)]
            for i in range(2):
                for j in range(4):
                    WD4[i][j].dsem = wd_ds[i * 4 + j]
            o = 32768
            SG = [[M.tile(M.view(o + (i * 2 + j) * 2048, [128, 512], F32)) for j in range(2)] for i in range(2)]
            o += 8192
            M1 = [[M.tile(M.view(o + (i * 2 + j) * 2048, [128, 512], F32)) for j in range(2)] for i in range(2)]
            o += 8192
            assert o <= M.nbytes

            def load_d(fp):
                sl = fp % 2
                sl_c = slice(fp * 256, (fp + 1) * 256)
                load_w(WD4[sl][0], WD4[sl][0].ap, wa_v[:, :, sl_c])
                load_w(WD4[sl][1], WD4[sl][1].ap, wb_v[:, :, sl_c])
                load_w(WD4[sl][2], WD4[sl][2].ap, win_v[:, :, 6144 + fp * 256: 6144 + (fp + 1) * 256])
                load_w(WD4[sl][3], WD4[sl][3].ap, win_v[:, :, 7168 + fp * 256: 7168 + (fp + 1) * 256])

            load_d(0)
            didx = 0
            for f in range(KC):
                fp = f // 2
                if f % 2 == 0 and fp + 1 < KC // 2:
                    load_d(fp + 1)
                Wa_, Wb_, Wga_, Wgb_ = WD4[fp % 2]
                cs = slice((f % 2) * 128, (f % 2 + 1) * 128)
                for tc in range(TG):
                    ts = slice(tc * 512, (tc + 1) * 512)
                    bks = [(didx % 2) * 4 + i for i in range(4)]
                    mmgroup(PS[:, bks[0], :], [(Wga_.ap[:, kc, cs], hT[kc][0][:, ts]) for kc in range(KC)],
                            BK[bks[0]], [hT[kc][1][tc] for kc in range(KC)] + [Wga_])
                    mmgroup(PS[:, bks[1], :], [(Wgb_.ap[:, kc, cs], hT[kc][0][:, ts]) for kc in range(KC)],
                            BK[bks[1]], [hT[kc][1][tc] for kc in range(KC)] + [Wgb_])
                    mmgroup(PS[:, bks[2], :], [(Wa_.ap[:, kc, cs], ya_ap[:, kc, ts]) for kc in range(KC)],
                            BK[bks[2]], [yaT[kc][tc] for kc in range(KC)] + [Wa_])
                    mmgroup(PS[:, bks[3], :], [(Wb_.ap[:, kc, cs], oT_ap[:, kc, ts]) for kc in range(KC)],
                            BK[bks[3]], [oT[kc][tc] for kc in range(KC)] + [Wb_])
                    sga, sgb = SG[didx % 2]
                    m1, m2 = M1[didx % 2]
                    S.op(ACT, lambda: nc.scalar.activation(out=sga.ap, in_=PS[:, bks[0], :], func=AF.Sigmoid),
                         reads=[BK[bks[0]]], writes=[sga])
                    S.op(ACT, lambda: nc.scalar.activation(out=sgb.ap, in_=PS[:, bks[1], :], func=AF.Sigmoid),
                         reads=[BK[bks[1]]], writes=[sgb])
                    S.op(DVE, lambda: nc.vector.tensor_tensor(out=m1.ap, in0=PS[:, bks[2], :], in1=sga.ap,
                                                              op=ALU.mult), reads=[BK[bks[2]], sga], writes=[m1])
                    S.op(DVE, lambda: nc.vector.tensor_tensor(out=m2.ap, in0=PS[:, bks[3], :], in1=sgb.ap,
                                                              op=ALU.mult), reads=[BK[bks[3]], sgb], writes=[m2])
                    S.op(POOL, lambda: nc.gpsimd.tensor_tensor(out=mT_ap[:, f, ts], in0=m1.ap, in1=m2.ap,
                                                               op=ALU.add), reads=[m1, m2], writes=[mT[f][tc]])
                    didx += 1

            if b == 0:
                dump("ya", ya_ap, [t_ for f_ in range(KC) for t_ in yaT[f_]])
                dump("mT", mT_ap, [t_ for f_ in range(KC) for t_ in mT[f_]])
            R1.reset()
            R2.reset()
            M.reset()
            x1a = R1.view(0, [128, TB // 2, D], F32)
            x1b = R2.view(0, [128, TB // 2, D], F32)

            def x1_ap(tb):
                return (x1a if tb < TB // 2 else x1b)[:, tb % (TB // 2), :]

            X1 = [(R1 if tb < TB // 2 else R2).tile(x1_ap(tb)) for tb in range(TB)]
            ya_dead = S.retire([yaT[f][g] for f in range(KC) for g in range(TG)])
            WO = T(R34.view(0, [128, KC, D], BF16), ya_dead)
            WO.dsem = wo_ds
            R34.tiles.append(WO)
            for hf in range(2):
                load_w(WO, WO.ap[:, :, hf * 512:(hf + 1) * 512], wo_v[:, :, hf * 512:(hf + 1) * 512])
            XIN = [M.tile(M.view(i * 4096, [128, 1024], F32)) for i in range(2)]
            for i in range(2):
                XIN[i].dsem = xin_ds[i]
            TMP = [M.tile(M.view(8192 + i * 2048, [128, 512], F32)) for i in range(2)]
            G1BC = M.tile(M.view(12288, [128, D], F32))
            G1BC.dsem = g_ds[0]
            S.dma(SP, G1BC.ap, gbc_d[b, 0], G1BC.dsem, reads=[GBC_D[b][0]], writes=[G1BC])
            eidx = 0
            for tb in range(TB):
                xin = XIN[tb % 2]
                S.dma(SP, xin.ap, x_d[b, tb * 128:(tb + 1) * 128, :], xin.dsem, writes=[xin])
                for hf in range(2):
                    b_ = nextbank()
                    hs = slice(hf * 512, (hf + 1) * 512)
                    mmgroup(PS[:, b_, :], [(mT_ap[:, kc, tb * 128:(tb + 1) * 128], WO.ap[:, kc, hs])
                                           for kc in range(KC)], BK[b_],
                            [mT[kc][tb // 4] for kc in range(KC)] + [WO])
                    tmp = TMP[eidx % 2]
                    S.op(DVE, lambda: nc.vector.tensor_tensor(out=tmp.ap, in0=PS[:, b_, :], in1=G1BC.ap[:, hs],
                                                              op=ALU.mult), reads=[BK[b_], G1BC], writes=[tmp])
                    S.op(POOL, lambda: nc.gpsimd.tensor_tensor(out=x1_ap(tb)[:, hs], in0=tmp.ap, in1=xin.ap[:, hs],
                                                               op=ALU.add), reads=[tmp, xin], writes=[X1[tb]])
                    eidx += 1

            if b == 0:
                dump("x1a", x1a, X1[:TB // 2], F32)
                dump("x1b", x1b, X1[TB // 2:], F32)
            R34.reset()
            M.reset()
            TMP = [M.tile(M.view(i * 1024, [128, 256], F32)) for i in range(2)]
            SGB_ = [M.tile(M.view(2048 + i * 2048, [128, 512], F32)) for i in range(2)]
            G2BC = M.tile(M.view(6144, [128, D], F32))
            G2BC.dsem = g_ds[1]
            S.dma(SP, G2BC.ap, gbc_d[b, 1], G2BC.dsem, reads=[GBC_D[b][1]], writes=[G2BC])
            XN = [M.tile(M.view(10240 + i * 2048, [128, 1024], BF16)) for i in range(4)]
            WG = [M.tile(M.view(18432 + i * 8192, [128, KC, 2, 256], BF16)) for i in range(2)]
            for i in range(2):
                WG[i].dsem = wg_ds[i]
            o = 18432 + 16384
            WDN = [M.tile(M.view(o + i * 11264, [128, FC, 256], BF16)) for i in range(2)]
            for i in range(2):
                WDN[i].dsem = wdn_ds[i]
            o += 22528
            assert o <= M.nbytes, o
            NCH = SQ // HS
            h2_ap = R34.view(0, [128, KC, HS], BF16)
            aT_ap = R34.view(16 * HS, [128, FC, HS], BF16)
            NG = HS // 512

            def load_gu(gp):
                t_ = WG[gp % 2]
                load_w(t_, t_.ap[:, :, 0, :], wgu_v[:, :, gp * 256:(gp + 1) * 256])
                load_w(t_, t_.ap[:, :, 1, :], wgu_v[:, :, DFF + gp * 256: DFF + (gp + 1) * 256])

            def load_dn(cp):
                t_ = WDN[cp % 2]
                load_w(t_, t_.ap[:, 0:11, :], wdn_v[:, 0:11, cp * 256:(cp + 1) * 256])
                load_w(t_, t_.ap[:, 11:22, :], wdn_v[:, 11:22, cp * 256:(cp + 1) * 256])

            def mk_tiles(aps, deps):
                ts_ = [T(ap, deps) for ap in aps]
                R34.tiles.extend(ts_)
                return ts_

            def mk_h2T(deps):
                return [(h2_ap[:, kc, :], mk_tiles([h2_ap[:, kc, g * 512:(g + 1) * 512] for g in range(NG)], deps))
                        for kc in range(KC)]

            def mk_aT(deps):
                return [mk_tiles([aT_ap[:, fc, g * 512:(g + 1) * 512] for g in range(NG)], deps)
                        for fc in range(FC)]

            def norm_part(ch, g):
                tb0 = ch * (HS // 128)
                return [norm_tile(X1[tb0 + g * 4 + i], x1_ap(tb0 + g * 4 + i), SSQ[i], XN[i]) for i in range(4)]

            def trans_part(h2T, g, xns):
                transpose_group(xns, h2T, G2s, S2s, b, g * 512, (tgc[0] % 2) * 4)
                tgc[0] += 1

            def gu_phase(h2T, aT, last):
                gidx = 0
                for gp in range(FC // 2):
                    if gp + 1 < FC // 2:
                        load_gu(gp + 1)
                    else:
                        load_dn(0)
                    wg = WG[gp % 2]
                    for f2 in range(2):
                        fc = gp * 2 + f2
                        for g in range(NG):
                            ts = slice(g * 512, (g + 1) * 512)
                            bks = [(gidx % 4) * 2, (gidx % 4) * 2 + 1]
                            for u_ in range(2):
                                mmgroup(PS[:, bks[u_], :],
                                        [(wg.ap[:, kc, u_, f2 * 128:(f2 + 1) * 128], h2T[kc][0][:, ts])
                                         for kc in range(KC)], BK[bks[u_]],
                                        [h2T[kc][1][g] for kc in range(KC)] + [wg])
                            sg = SGB_[gidx % 2]
                            S.op(ACT, lambda: nc.scalar.activation(out=sg.ap, in_=PS[:, bks[0], :], func=AF.Silu),
                                 reads=[BK[bks[0]]], writes=[sg])
                            S.op(DVE, lambda: nc.vector.tensor_tensor(out=aT_ap[:, fc, ts], in0=PS[:, bks[1], :],
                                                                      in1=sg.ap, op=ALU.mult),
                                 reads=[BK[bks[1]], sg], writes=[aT[fc][g]])
                            gidx += 1

            def down_phase(ch, aT, hooks):
                nonlocal_e = ecnt
                tb0 = ch * (HS // 128)
                for cp in range(4):
                    if cp + 1 < 4:
                        load_dn(cp + 1)
                    wd = WDN[cp % 2]
                    cs = slice(cp * 256, (cp + 1) * 256)
                    for tbl in range(HS // 128):
                        tb = tb0 + tbl
                        b_ = nextbank()
                        mmgroup(PS[:, b_, 0:256], [(aT_ap[:, fc, tbl * 128:(tbl + 1) * 128], wd.ap[:, fc, :])
                                                   for fc in range(FC)], BK[b_],
                                [aT[fc][tbl // 4] for fc in range(FC)] + [wd])
                        tmp = TMP[nonlocal_e[0] % 2]
                        S.op(DVE, lambda: nc.vector.tensor_tensor(out=tmp.ap[:, 0:256], in0=PS[:, b_, 0:256],
                                                                  in1=G2BC.ap[:, cs], op=ALU.mult),
                             reads=[BK[b_], G2BC], writes=[tmp])
                        S.op(POOL, lambda: nc.gpsimd.tensor_tensor(out=x1_ap(tb)[:, cs], in0=tmp.ap[:, 0:256],
                                                                   in1=x1_ap(tb)[:, cs], op=ALU.add),
                             reads=[tmp, X1[tb]], writes=[X1[tb]])
                        nonlocal_e[0] += 1
                        if cp == 3:
                            S.dma(SP, out_d[b, tb * 128:(tb + 1) * 128, :], x1_ap(tb), out_ds[tb], reads=[X1[tb]])
                    for hk in hooks.get(cp, []):
                        hk()

            ecnt = [0]
            h2T_c = mk_h2T(R34.fence)
            aT_c = mk_aT(R34.fence)
            load_gu(0)
            for g in range(NG):
                trans_part(h2T_c, g, norm_part(0, g))
            for ch in range(NCH):
                gu_phase(h2T_c, aT_c, ch == NCH - 1)
                hooks = {}
                if ch + 1 < NCH:
                    h2T_n = mk_h2T(S.retire([t_ for kc in range(KC) for t_ in h2T_c[kc][1]]))
                    pend = {}
                    pend[0] = norm_part(ch + 1, 0)

                    def mk_hook(g, h2T_n=h2T_n, pend=pend, ch=ch):
                        def hk():
                            trans_part(h2T_n, g, pend[g])
                            if g + 1 < NG:
                                pend[g + 1] = norm_part(ch + 1, g + 1)
                            else:
                                load_gu(0)
                        return hk
                    for g in range(NG):
                        hooks.setdefault(min(g, 3), []).append(mk_hook(g))
                down_phase(ch, aT_c, hooks)
                if ch + 1 < NCH:
                    aT_c = mk_aT(S.retire([t_ for fc in range(FC) for t_ in aT_c[fc]]))
                    h2T_c = h2T_n
            R34.reset()

        for ds in out_ds + [dbg_ds]:
            if ds.val:
                SP.eng.wait_ge(ds.sem, ds.val)
    return nc


_CACHE = {}


def _layout_inputs(inp, NB):
    f = lambda a: np.ascontiguousarray(np.asarray(a, dtype=np.float32))
    x = f(inp["x"])
    c = f(inp["c"])
    shared = {
        "w_ada": f(inp["w_ada"][0]),
        "b_adaT": f(inp["b_ada"][0].reshape(48, 128).T),
        "b_ada_row": f(inp["b_ada"][0].reshape(1, 6 * D)),
        "n1gT": f(inp["norm1_g"][0].reshape(KC, 128).T),
        "n2gT": f(inp["norm2_g"][0].reshape(KC, 128).T),
        "w_in": f(inp["w_in"][0]),
        "cwT": f(np.asarray(inp["conv_w"][0]).reshape(3, KC, 128).transpose(2, 0, 1)),
        "gq": f(np.tile(np.asarray(inp["q_norm_g"][0]), 2).reshape(128, 1)),
        "gk": f(np.tile(np.asarray(inp["k_norm_g"][0]), 2).reshape(128, 1)),
        "lamv": f(np.concatenate([np.asarray(inp[k_][0]) for k_ in
                                  ("lambda_q1", "lambda_k1", "lambda_q2", "lambda_k2")]).reshape(1, 256)),
        "subg": f(np.asarray(inp["subln_g"][0]).reshape(128, 1)),
        "w_a_out": f(inp["w_a_out"][0]),
        "w_b_out": f(inp["w_b_out"][0]),
        "w_o": f(inp["w_o"][0]),
        "w_gu": f(inp["w_gu"][0]),
        "w_down": f(inp["w_down"][0]),
    }
    n_cores = x.shape[0] // NB
    maps = []
    for i in range(n_cores):
        m = dict(shared)
        m["x"] = np.ascontiguousarray(x[i * NB:(i + 1) * NB])
        cs = c[i * NB:(i + 1) * NB]
        m["cT"] = np.ascontiguousarray(cs.reshape(NB, KC, 128).transpose(2, 1, 0))
        maps.append(m)
    return maps


def kernel(**inputs):
    x = np.asarray(inputs["x"])
    B, SQ, _ = x.shape
    NB = B // N_CORES
    key = (NB, SQ)
    if key not in _CACHE:
        _CACHE[key] = build(NB, SQ)
    nc = _CACHE[key]
    maps = _layout_inputs(inputs, NB)
    res = run_bass_kernel_spmd(nc, maps, core_ids=list(range(N_CORES)))
    out = np.concatenate([np.asarray(r["out"]) for r in res.results], axis=0)
    return out.astype(np.float32, copy=False)
```

```python
import contextlib
import numpy as np
import concourse.bass as bass
import concourse.mybir as mybir
from concourse.bass_utils import run_bass_kernel_spmd

F32 = mybir.dt.float32
BF16 = mybir.dt.bfloat16
AF = mybir.ActivationFunctionType
ALU = mybir.AluOpType
AX = mybir.AxisListType

D = 1024
KC = 8
H = 8
DFF = 2816
FC = 22
EPS = 1e-6
LAMBDA_INIT = 0.2
N_CORES = 8
INTERLEAVE_PROJ = True
POOL_OFFLOAD = False


class T:
    __slots__ = ("ap", "w", "r", "dsem")

    def __init__(self, ap, deps=None):
        self.ap = ap
        self.w = dict(deps) if deps else {}
        self.r = {}
        self.dsem = None


class DSem:
    def __init__(self, sem):
        self.sem = sem
        self.val = 0
        self.key = "d%d" % id(self)


class Eng:
    def __init__(self, name, eng, sem, self_sync):
        self.name = name
        self.eng = eng
        self.sem = sem
        self.count = 0
        self.waited = {}
        self.self_sync = self_sync


class Sched:
    def __init__(self, nc, es):
        self.nc = nc
        self.es = es
        self.nsem = 0

        def mk(name, eng, ss):
            return Eng(name, eng, self.newsem("e_" + name), ss)

        self.pe = mk("pe", nc.tensor, False)
        self.act = mk("act", nc.scalar, True)
        self.dve = mk("dve", nc.vector, True)
        self.pool = mk("pool", nc.gpsimd, True)
        self.sp = mk("sp", nc.sync, True)

    def newsem(self, name):
        self.nsem += 1
        return self.es.enter_context(self.nc.semaphore(name))

    def dsem(self, name):
        return DSem(self.newsem(name))

    @staticmethod
    def _merge(d, src):
        for k, v in src.items():
            o = d.get(k)
            if o is None or o[1] < v[1]:
                d[k] = v

    def _wait(self, E, deps):
        for k, (sem, val) in deps.items():
            if k == E.name and not E.self_sync:
                continue
            if E.waited.get(k, 0) >= val:
                continue
            E.eng.wait_ge(sem, val)
            E.waited[k] = val

    def op(self, E, fn, reads=(), writes=(), signal=True):
        deps = {}
        for t in reads:
            self._merge(deps, t.w)
        for t in writes:
            self._merge(deps, t.w)
            self._merge(deps, t.r)
        self._wait(E, deps)
        ins = fn()
        if signal:
            E.count += 1
            ins.then_inc(E.sem, 1)
            tk = (E.sem, E.count)
        else:
            tk = (E.sem, E.count + 1)
        k = E.name
        for t in reads:
            o = t.r.get(k)
            if o is None or o[1] < tk[1]:
                t.r[k] = tk
        for t in writes:
            t.w = {k: tk}
            t.r = {}
        return ins

    def dma(self, Q, out_ap, in_ap, ds, reads=(), writes=()):
        deps = {}
        for t in reads:
            self._merge(deps, t.w)
        for t in writes:
            self._merge(deps, t.w)
            self._merge(deps, t.r)
        self._wait(Q, deps)
        ins = Q.eng.dma_start(out=out_ap, in_=in_ap)
        ds.val += 16
        ins.then_inc(ds.sem, 16)
        tk = (ds.sem, ds.val)
        k = ds.key
        for t in reads:
            o = t.r.get(k)
            if o is None or o[1] < tk[1]:
                t.r[k] = tk
        for t in writes:
            t.w = {k: tk}
            t.r = {}
        return ins

    def retire(self, tiles):
        deps = {}
        for t in tiles:
            self._merge(deps, t.w)
            self._merge(deps, t.r)
        return deps


class Region:
    def __init__(self, S_, nc, es, name, nbytes):
        self.S = S_
        self.nbytes = nbytes
        self.t = es.enter_context(nc.sbuf_tensor(name, [128, nbytes // 2], BF16))
        self.tiles = []
        self.fence = {}

    def view(self, off, shape, dt):
        n = int(np.prod(shape[1:]))
        nb = n * (4 if dt == F32 else 2)
        assert off % 4 == 0 and off + nb <= self.nbytes, (off, nb, self.nbytes)
        ap = self.t[:, off // 2:(off + nb) // 2]
        if dt == F32:
            ap = ap.bitcast(F32)
        if len(shape) == 3:
            ap = ap.rearrange("p (a b) -> p a b", b=shape[2])
        elif len(shape) == 4:
            ap = ap.rearrange("p (a b c) -> p a b c", b=shape[2], c=shape[3])
        return ap

    def tile(self, ap):
        t = T(ap, self.fence)
        self.tiles.append(t)
        return t

    def reset(self):
        f = self.S.retire(self.tiles)
        self.S._merge(f, self.fence)
        self.fence = f
        self.tiles = []


def build(NB=4, SQ=2048, debug=False):
    TB = SQ // 128
    TG = SQ // 512
    HS = max(512, SQ // 2)
    nc = bass.Bass("TRN2", target_bir_lowering=False)

    def din(n, sh):
        return nc.dram_tensor(n, sh, F32, kind="ExternalInput").ap()

    x_d = din("x", [NB, SQ, D])
    cT_d = din("cT", [128, KC, NB])
    wada_d = din("w_ada", [D, 6 * D])
    badaT_d = din("b_adaT", [128, 48])
    badar_d = din("b_ada_row", [1, 6 * D])
    n1g_d = din("n1gT", [128, KC])
    n2g_d = din("n2gT", [128, KC])
    win_d = din("w_in", [D, 8 * D])
    cw_d = din("cwT", [128, 3, KC])
    gq_d = din("gq", [128, 1])
    gk_d = din("gk", [128, 1])
    lamv_d = din("lamv", [1, 256])
    subg_d = din("subg", [128, 1])
    wa_d = din("w_a_out", [D, D])
    wb_d = din("w_b_out", [D, D])
    wo_d = din("w_o", [D, D])
    wgu_d = din("w_gu", [D, 2 * DFF])
    wdn_d = din("w_down", [DFF, D])
    out_d = nc.dram_tensor("out", [NB, SQ, D], F32, kind="ExternalOutput").ap()
    gbc_d = nc.dram_tensor("gbc", [NB, 2, 128, D], F32, kind="Internal").ap()

    kv = "(kc p) n -> p kc n"
    wada_v = wada_d.rearrange(kv, p=128)
    win_v = win_d.rearrange(kv, p=128)
    wa_v = wa_d.rearrange(kv, p=128)
    wb_v = wb_d.rearrange(kv, p=128)
    wo_v = wo_d.rearrange(kv, p=128)
    wgu_v = wgu_d.rearrange(kv, p=128)
    wdn_v = wdn_d.rearrange(kv, p=128)

    with contextlib.ExitStack() as es:
        S = Sched(nc, es)
        PE, ACT, DVE, POOL, SP = S.pe, S.act, S.dve, S.pool, S.sp

        def const(name, shape, dt):
            return T(es.enter_context(nc.sbuf_tensor("c_" + name, shape, dt))[:])

        R1 = Region(S, nc, es, "R1", 16 * SQ)
        R2 = Region(S, nc, es, "R2", 16 * SQ)
        R34_MAIN = max(32 * SQ, TB * 2080 + 64 + 28672 + 512)
        R34 = Region(S, nc, es, "R34", R34_MAIN + 16384)
        M = Region(S, nc, es, "M", 56 * 1024)
        ONESF = M.tile(M.view(49408, [128, 128], F32))
        ZEROF = M.tile(M.view(49920, [128, 128], F32))
        LAMV = M.tile(M.view(50432, [128, 256], F32))
        LTMP = M.tile(M.view(51456, [128, 128], F32))
        PS = es.enter_context(nc.psum_tensor("PS", [128, 8, 512], F32))
        BK = [T(PS[:, i, :]) for i in range(8)]

        def bank_bf(i):
            return PS[:, i, :].bitcast(BF16)

        IDENT = const("ident", [128, 128], BF16)
        TRI = const("tri", [128, 128], BF16)
        BLK = const("blk", [128, 128], BF16)
        MHALF = const("mhalf", [128, 8], F32)
        G1s = const("G1s", [128, KC, NB], F32)
        S1s = const("S1s", [128, KC, NB], F32)
        G2s = const("G2s", [128, KC, NB], F32)
        S2s = const("S2s", [128, KC, NB], F32)
        N1G = const("n1g", [128, KC], F32)
        N2G = const("n2g", [128, KC], F32)
        BADAT = const("badaT", [128, 48], F32)
        CW = const("cw", [128, 3, KC], F32)
        GQ = const("gq", [128, 1], F32)
        GK = const("gk", [128, 1], F32)
        SUBG = const("subg", [128, 1], F32)
        NEGLAM = const("neglam", [128, 1], F32)
        CT = const("ct", [128, KC, NB], F32)
        SCT = const("sct", [128, KC, NB], BF16)
        LS = const("ls", [128, 2], F32)
        SSQ = [const("ssq%d" % i, [128, 1], F32) for i in range(4)]
        EPSC = const("epsc", [128, 1], F32)
        JUNK = es.enter_context(nc.sbuf_tensor("junk", [128, 1024], BF16))

        dsetup = [S.dsem("dsu%d" % i) for i in range(10)]
        k = 0
        for t_, src in ((CT, cT_d), (BADAT, badaT_d), (N1G, n1g_d), (N2G, n2g_d), (CW, cw_d), (GQ, gq_d),
                        (GK, gk_d), (SUBG, subg_d)):
            S.dma(SP, t_.ap, src, dsetup[k], writes=[t_])
            k += 1
        S.dma(SP, LAMV.ap, lamv_d.partition_broadcast(128), dsetup[k], writes=[LAMV])

        S.op(POOL, lambda: nc.gpsimd.memset(ONESF.ap, 1.0), writes=[ONESF])
        S.op(POOL, lambda: nc.gpsimd.memset(MHALF.ap, -0.5), writes=[MHALF])
        S.op(POOL, lambda: nc.gpsimd.memset(EPSC.ap, EPS), writes=[EPSC])
        S.op(POOL, lambda: nc.gpsimd.affine_select(out=IDENT.ap, in_=ONESF.ap, pattern=[[-1, 128]],
                                                   compare_op=ALU.is_equal, fill=0.0, base=0,
                                                   channel_multiplier=1), reads=[ONESF], writes=[IDENT])
        S.op(POOL, lambda: nc.gpsimd.memset(ZEROF.ap, 0.0), writes=[ZEROF])
        S.op(POOL, lambda: nc.gpsimd.affine_select(out=TRI.ap, in_=ZEROF.ap, pattern=[[1, 128]],
                                                   compare_op=ALU.is_ge, fill=-30000.0, base=0,
                                                   channel_multiplier=-1), reads=[ZEROF], writes=[TRI])
        S.op(POOL, lambda: nc.gpsimd.memset(BLK.ap, 0.0), writes=[BLK])
        S.op(POOL, lambda: nc.gpsimd.memset(BLK.ap[0:64, 0:64], 1.0), writes=[BLK])
        S.op(POOL, lambda: nc.gpsimd.memset(BLK.ap[64:128, 64:128], 1.0), writes=[BLK])
        S.op(DVE, lambda: nc.vector.tensor_scalar(out=SUBG.ap, in0=SUBG.ap, scalar1=1.0 - LAMBDA_INIT,
                                                  scalar2=None, op0=ALU.mult), reads=[SUBG], writes=[SUBG])
        S.op(DVE, lambda: nc.vector.tensor_tensor(out=LTMP.ap[:, 0:64], in0=LAMV.ap[:, 0:64],
                                                  in1=LAMV.ap[:, 64:128], op=ALU.mult), reads=[LAMV], writes=[LTMP])
        S.op(DVE, lambda: nc.vector.tensor_tensor(out=LTMP.ap[:, 64:128], in0=LAMV.ap[:, 128:192],
                                                  in1=LAMV.ap[:, 192:256], op=ALU.mult), reads=[LAMV], writes=[LTMP])
        S.op(DVE, lambda: nc.vector.reduce_sum(out=LS.ap, in_=LTMP.ap.rearrange("p (a b) -> p a b", b=64),
                                               axis=AX.X), reads=[LTMP], writes=[LS])
        S.op(ACT, lambda: nc.scalar.activation(out=LS.ap, in_=LS.ap, func=AF.Exp), reads=[LS], writes=[LS])
        S.op(DVE, lambda: nc.vector.tensor_tensor(out=NEGLAM.ap, in0=LS.ap[:, 1:2], in1=LS.ap[:, 0:1],
                                                  op=ALU.subtract), reads=[LS], writes=[NEGLAM])
        S.op(DVE, lambda: nc.vector.tensor_scalar(out=NEGLAM.ap, in0=NEGLAM.ap, scalar1=-LAMBDA_INIT,
                                                  scalar2=None, op0=ALU.add), reads=[NEGLAM], writes=[NEGLAM])
        S.op(ACT, lambda: nc.scalar.activation(out=SCT.ap, in_=CT.ap, func=AF.Silu), reads=[CT], writes=[SCT])

        WADA = [M.tile(M.view(i * 16384, [128, KC, 1024], BF16)) for i in range(2)]
        for i_, t_ in enumerate(WADA):
            t_.dsem = S.dsem("dwada%d" % i_)
        BBC = M.tile(M.view(32768, [128, 1024], F32))
        BBC.dsem = S.dsem("dbbc")
        GST = [M.tile(M.view(36864 + i * 4096, [128, 1024], F32)) for i in range(2)]
        SCB = [M.tile(M.view(45056 + i * 2048, [128, KC, 128], BF16)) for i in range(2)]
        MODT = M.tile(M.view(49152, [128, KC, NB], F32))
        gst_ds = [S.dsem("dgst%d" % i) for i in range(2)]
        GBC_D = [[T(gbc_d[b, w]) for w in range(2)] for b in range(NB)]

        def load_wada(j, pos):
            t_ = WADA[pos % 2]
            for hf in range(2):
                S.dma(POOL, t_.ap[:, :, hf * 512:(hf + 1) * 512],
                      wada_v[:, :, j * 1024 + hf * 512: j * 1024 + (hf + 1) * 512], t_.dsem, writes=[t_])
            return t_

        nbk = [0]

        def nextbank():
            b = nbk[0] % 8
            nbk[0] += 1
            return b

        order = [0, 1, 3, 4, 2, 5]
        cur = load_wada(order[0], 0)
        gcount = 0
        for oi, j in enumerate(order):
            nxt = load_wada(order[oi + 1], oi + 1) if oi + 1 < len(order) else None
            if j in (0, 1, 3, 4):
                b_ = nextbank()
                for fcn in range(KC):
                    for kc in range(KC):
                        S.op(PE, lambda: nc.tensor.matmul(PS[:, b_, fcn * NB:(fcn + 1) * NB],
                                                          lhsT=cur.ap[:, kc, fcn * 128:(fcn + 1) * 128],
                                                          rhs=SCT.ap[:, kc, :], start=(kc == 0), stop=(kc == KC - 1),
                                                          skip_group_check=True),
                             reads=[cur, SCT], writes=[BK[b_]], signal=(kc == KC - 1))
                psv = PS[:, b_, 0:KC * NB].rearrange("p (a b) -> p a b", b=NB)
                bias = BADAT.ap[:, j * 8:(j + 1) * 8].unsqueeze(2).to_broadcast([128, KC, NB])
                dst = {0: S1s, 1: MODT, 3: S2s, 4: MODT}[j]
                S.op(DVE, lambda: nc.vector.tensor_tensor(out=dst.ap, in0=psv, in1=bias, op=ALU.add),
                     reads=[BK[b_], BADAT], writes=[dst])
                if j in (1, 4):
                    gt, ng = (G1s, N1G) if j == 1 else (G2s, N2G)
                    S.op(DVE, lambda: nc.vector.scalar_tensor_tensor(
                        out=gt.ap, in0=MODT.ap, scalar=1.0,
                        in1=ng.ap.unsqueeze(2).to_broadcast([128, KC, NB]), op0=ALU.add, op1=ALU.mult),
                        reads=[MODT, ng], writes=[gt])
            else:
                w_ = 0 if j == 2 else 1
                S.dma(SP, BBC.ap, badar_d[:, j * 1024:(j + 1) * 1024].partition_broadcast(128), BBC.dsem,
                      writes=[BBC])
                for b in range(NB):
                    scb = SCB[gcount % 2]
                    gst = GST[gcount % 2]
                    S.op(DVE, lambda: nc.vector.tensor_copy(
                        out=scb.ap, in_=SCT.ap[:, :, b:b + 1].to_broadcast([128, KC, 128])),
                        reads=[SCT], writes=[scb])
                    for hf in range(2):
                        b_ = nextbank()
                        for kc in range(KC):
                            S.op(PE, lambda: nc.tensor.matmul(PS[:, b_, :], lhsT=scb.ap[:, kc, :],
                                                              rhs=cur.ap[:, kc, hf * 512:(hf + 1) * 512],
                                                              start=(kc == 0), stop=(kc == KC - 1)),
                                 reads=[scb, cur], writes=[BK[b_]], signal=(kc == KC - 1))
                        S.op(DVE, lambda: nc.vector.tensor_tensor(out=gst.ap[:, hf * 512:(hf + 1) * 512],
                                                                  in0=PS[:, b_, :],
                                                                  in1=BBC.ap[:, hf * 512:(hf + 1) * 512],
                                                                  op=ALU.add),
                             reads=[BK[b_], BBC], writes=[gst])
                    S.dma(SP, gbc_d[b, w_], gst.ap, gst_ds[gcount % 2], reads=[gst], writes=[GBC_D[b][w_]])
                    gcount += 1
            cur = nxt
        M.reset()

        out_ds = [S.dsem("dout%d" % i) for i in range(TB)]
        dbg_ds = S.dsem("ddbg")

        def dump(name, ap, tiles, dt=BF16):
            if not debug:
                return
            d_ = nc.dram_tensor("dbg_" + name, list(ap.shape), dt, kind="ExternalOutput").ap()
            S.dma(SP, d_, ap, dbg_ds, reads=tiles)

        for nm, t_ in (("G1s", G1s), ("S1s", S1s), ("G2s", G2s), ("S2s", S2s), ("NEGLAM", NEGLAM), ("SUBG", SUBG)):
            dump(nm, t_.ap, [t_], F32)
        xin_ds = [S.dsem("dxin%d" % i) for i in range(8)]
        wv_ds = [S.dsem("dwv%d" % i) for i in range(2)]
        wq_ds = [S.dsem("dwq%d" % i) for i in range(4)]
        wc_ds = [S.dsem("dwc%d" % i) for i in range(6)]
        wd_ds = [S.dsem("dwd%d" % i) for i in range(8)]
        wo_ds = S.dsem("dwo")
        wg_ds = [S.dsem("dwg%d" % i) for i in range(2)]
        wdn_ds = [S.dsem("dwdn%d" % i) for i in range(2)]
        g_ds = [S.dsem("dg%d" % i) for i in range(2)]

        def evac_mod(idx, out_ap, in_ap, g_ap, s_ap, reads, writes):
            if idx % 2 == 0:
                S.op(ACT, lambda: nc.scalar.activation(out=out_ap, in_=in_ap, func=AF.Identity, scale=g_ap,
                                                       bias=s_ap), reads=reads, writes=writes)
            else:
                S.op(DVE, lambda: nc.vector.tensor_scalar(out=out_ap, in0=in_ap, scalar1=g_ap, scalar2=s_ap,
                                                          op0=ALU.mult, op1=ALU.add), reads=reads, writes=writes)

        def rstd_from_ssq(ssq, n):
            S.op(DVE, lambda: nc.vector.tensor_scalar(out=ssq.ap, in0=ssq.ap, scalar1=1.0 / n, scalar2=EPS,
                                                      op0=ALU.mult, op1=ALU.add), reads=[ssq], writes=[ssq])
            S.op(POOL, lambda: nc.gpsimd.tensor_tensor(out=ssq.ap, in0=ssq.ap, in1=MHALF.ap[:, 0:1], op=ALU.pow),
                 reads=[ssq, MHALF], writes=[ssq])

        def norm_tile(src_t, src_ap, ssq, xn):
            S.op(ACT, lambda: nc.scalar.activation(out=JUNK[:, :], in_=src_ap, func=AF.Square,
                                                   accum_out=ssq.ap), reads=[src_t], writes=[ssq])
            rstd_from_ssq(ssq, D)
            S.op(DVE, lambda: nc.vector.tensor_scalar(out=xn.ap, in0=src_ap, scalar1=ssq.ap, scalar2=None,
                                                      op0=ALU.mult), reads=[src_t, ssq], writes=[xn])
            return xn

        def transpose_group(xns, dstT, Gs, Ss, b, col0, bank_base):
            for kc in range(KC):
                b_ = bank_base + kc // 2
                pv = bank_bf(b_)
                c0 = (kc % 2) * 512
                for i in range(4):
                    S.op(PE, lambda: nc.tensor.transpose(pv[:, c0 + i * 128: c0 + (i + 1) * 128],
                                                         xns[i].ap[:, kc * 128:(kc + 1) * 128], IDENT.ap),
                         reads=[xns[i], IDENT], writes=[BK[b_]], signal=(i == 3))
                evac_mod(kc, dstT[kc][0][:, col0:col0 + 512], pv[:, c0:c0 + 512], Gs.ap[:, kc, b:b + 1],
                         Ss.ap[:, kc, b:b + 1], [BK[b_], Gs, Ss], [dstT[kc][1][col0 // 512]])

        def norm_transpose(src_tiles, XN, dstT, Gs, Ss, b, tg_global, col0):
            xns = [norm_tile(src_t, src_ap, SSQ[i], XN[(tg_global * 4 + i) % len(XN)])
                   for i, (src_t, src_ap) in enumerate(src_tiles)]
            transpose_group(xns, dstT, Gs, Ss, b, col0, (tg_global % 2) * 4)

        def load_w(t_, dst_ap, src_ap):
            S.dma(POOL, dst_ap, src_ap, t_.dsem, writes=[t_])

        def mmgroup(out_ap, pairs, bk, reads, **kw):
            n = len(pairs)
            for i, (l, r) in enumerate(pairs):
                S.op(PE, lambda: nc.tensor.matmul(out_ap, lhsT=l, rhs=r, start=(i == 0), stop=(i == n - 1), **kw),
                     reads=reads, writes=[bk], signal=(i == n - 1))

        tgc = [0]
        ecount = [0]
        WV = [T(R34.view(R34_MAIN + i * 8192, [128, KC, 512], BF16)) for i in range(2)]
        for i in range(2):
            WV[i].dsem = wv_ds[i]
            load_w(WV[i], WV[i].ap, win_v[:, :, 2048 + i * 512: 2048 + (i + 1) * 512])

        next_hT = [None]
        for b in range(NB):
            prefetched = next_hT[0] is not None
            if prefetched:
                hT_ap, hT = next_hT[0]
                next_hT[0] = None
            else:
                R1.reset()
                hT_ap = R1.view(0, [128, KC, SQ], BF16)
                hT = [(hT_ap[:, kc, :], [R1.tile(hT_ap[:, kc, g * 512:(g + 1) * 512]) for g in range(TG)])
                      for kc in range(KC)]
            M.reset()
            XIN = [M.tile(M.view(i * 4096, [128, 1024], F32)) for i in range(8)]
            for i in range(8):
                XIN[i].dsem = xin_ds[i]
            XN = [M.tile(M.view(32768 + i * 2048, [128, 1024], BF16)) for i in range(4)]
            R34.reset()
            R2.reset()
            VA_ap = R34.view(0, [128, TB, H, 130], BF16)
            VA = [[R34.tile(VA_ap[:, tb, hf * 4:(hf + 1) * 4, :]) for hf in range(2)] for tb in range(TB)]
            allva = [VA[tb][hf] for tb in range(TB) for hf in range(2)]
            S.op(POOL, lambda: nc.gpsimd.memset(VA_ap[:, :, :, 128:130], 1.0), writes=allva)

            vcnt = [0]

            def v_proj(tb):
                for hf in range(2):
                    b_ = 4 + vcnt[0] % 4
                    vcnt[0] += 1
                    mmgroup(PS[:, b_, :], [(hT[kc][0][:, tb * 128:(tb + 1) * 128], WV[hf].ap[:, kc, :])
                                           for kc in range(KC)], BK[b_],
                            [hT[kc][1][tb // 4] for kc in range(KC)] + [WV[hf]])
                    src = PS[:, b_, :].rearrange("p (a b) -> p a b", b=128)
                    dst = VA_ap[:, tb, hf * 4:(hf + 1) * 4, 0:128]
                    if hf == 0:
                        S.op(ACT, lambda: nc.scalar.copy(out=dst, in_=src), reads=[BK[b_]], writes=[VA[tb][hf]])
                    else:
                        S.op(DVE, lambda: nc.vector.tensor_copy(out=dst, in_=src), reads=[BK[b_]],
                             writes=[VA[tb][hf]])

            def load_x(tg):
                for i in range(4):
                    tb = tg * 4 + i
                    xin = XIN[(tg % 2) * 4 + i]
                    S.dma(SP, xin.ap, x_d[b, tb * 128:(tb + 1) * 128, :], xin.dsem, writes=[xin])

            if prefetched:
                for tb in range(TB):
                    v_proj(tb)
            else:
                load_x(0)
                for tg in range(TG):
                    if tg + 1 < TG:
                        load_x(tg + 1)
                    xns = []
                    for i in range(4):
                        xin = XIN[(tg % 2) * 4 + i]
                        xns.append(norm_tile(xin, xin.ap, SSQ[i], XN[i]))
                        if tg >= 1:
                            v_proj((tg - 1) * 4 + i)
                    transpose_group(xns, hT, G1s, S1s, b, tg * 512, 0)
                for i in range(4):
                    v_proj((TG - 1) * 4 + i)

            oT_ap = R2.view(0, [128, H, SQ], BF16)
            oT = [[R2.tile(oT_ap[:, h, g * 512:(g + 1) * 512]) for g in range(TG)] for h in range(H)]
            M.reset()
            WQ = [M.tile(M.view(i * 4096, [128, KC, 256], BF16)) for i in range(2)]
            WK = [M.tile(M.view(8192 + i * 4096, [128, KC, 256], BF16)) for i in range(2)]
            for i in range(2):
                WQ[i].dsem = wq_ds[i]
                WK[i].dsem = wq_ds[2 + i]
            QT = [M.view(16384 + i * 2 * SQ, [128, SQ], BF16) for i in range(2)]
            KT = [M.view(16384 + 4 * SQ + i * 2 * SQ, [128, SQ], BF16) for i in range(2)]
            QTt = [[M.tile(QT[i][:, g * 512:(g + 1) * 512]) for g in range(TG)] for i in range(2)]
            KTt = [[M.tile(KT[i][:, g * 512:(g + 1) * 512]) for g in range(TG)] for i in range(2)]
            o = 16384 + 8 * SQ
            ET = [M.tile(M.view(o + i * 2048, [128, 2, 512], BF16)) for i in range(3)]
            o += 6144
            o34 = (TB * 2080 + 63) // 64 * 64
            NQS = 8
            QS = [R34.tile(R34.view(o34 + i * 2048, [128, 512], F32)) for i in range(NQS)]
            o34 += 2048 * NQS
            SQB = [R34.tile(R34.view(o34 + i * 1024, [128, 512], BF16)) for i in range(NQS)]
            o34 += 1024 * NQS
            SDB = [R34.tile(R34.view(o34 + i * 2048, [128, 512], F32)) for i in range(2)]
            o34 += 4096
            RC = [M.tile(M.view(o + i * 32, [128, 8], F32)) for i in range(2)]
            o += 64
            SS4 = [M.tile(M.view(o + i * 32, [128, 4], F32)) for i in range(2)]
            o += 64
            AS = [M.tile(M.view(o + i * 4128, [128, 8, 129], F32)) for i in range(2)]
            o += 8256
            ON = [M.tile(M.view(o + i * 1024, [128, 4, 128], BF16)) for i in range(2)]
            o += 2048
            assert o <= M.nbytes, o

            def load_qk(hp):
                sl = hp % 2
                load_w(WQ[sl], WQ[sl].ap, win_v[:, :, hp * 256:(hp + 1) * 256])
                load_w(WK[sl], WK[sl].ap, win_v[:, :, 1024 + hp * 256: 1024 + (hp + 1) * 256])

            load_qk(0)
            nq = [0]
            acc_slots = [((4 + s_, il * 129) if il < 3 else (6, s_ * 129)) for il in range(4) for s_ in range(2)]
            deferred = []
            nchunk = [0]

            def flush_deferred():
                while deferred:
                    deferred.pop(0)()
            tiles_ = [(tc, w) for tc in range(TG) for w in range(2)]

            def make_units(h_):
                hp_, par_ = h_ // 2, h_ % 2

                def unit_a(t, b_, half=None):
                    tc, w = tiles_[t]
                    W_ = (WQ, WK)[w][hp_ % 2]
                    kcs = range(KC) if half is None else range(half * 4, half * 4 + 4)
                    for kc in kcs:
                        S.op(PE, lambda: nc.tensor.matmul(PS[:, b_, :],
                                                          lhsT=W_.ap[:, kc, (h_ % 2) * 128:(h_ % 2 + 1) * 128],
                                                          rhs=hT[kc][0][:, tc * 512:(tc + 1) * 512],
                                                          start=(kc == 0), stop=(kc == KC - 1)),
                             reads=[hT[kc][1][tc], W_], writes=[BK[b_]], signal=(kc == KC - 1))
                    if half == 0:
                        return
                    sq, qs = SQB[t % NQS], QS[t % NQS]
                    S.op(DVE, lambda: nc.vector.tensor_copy(out=qs.ap, in_=PS[:, b_, :]), reads=[BK[b_]],
                         writes=[qs])
                    E_, e_ = (POOL, nc.gpsimd) if POOL_OFFLOAD else (DVE, nc.vector)
                    S.op(E_, lambda: e_.tensor_tensor(out=sq.ap, in0=qs.ap, in1=qs.ap, op=ALU.mult),
                         reads=[qs], writes=[sq])

                def unit_b(t, b2):
                    tc, w = tiles_[t]
                    gvec = (GQ, GK)[w]
                    dst_ap = (QT, KT)[w][par_]
                    dst_t = (QTt, KTt)[w][par_]
                    sq, qs, sd = SQB[t % NQS], QS[t % NQS], SDB[t % 2]
                    S.op(PE, lambda: nc.tensor.matmul(PS[:, b2, :], lhsT=BLK.ap, rhs=sq.ap, start=True,
                                                      stop=True), reads=[BLK, sq], writes=[BK[b2]])
                    S.op(ACT, lambda: nc.scalar.activation(out=sd.ap, in_=PS[:, b2, :], func=AF.Ln,
                                                           scale=1.0 / 64, bias=EPSC.ap),
                         reads=[BK[b2], EPSC], writes=[sd])
                    S.op(ACT, lambda: nc.scalar.activation(out=sd.ap, in_=sd.ap, func=AF.Exp, scale=-0.5),
                         reads=[sd], writes=[sd])
                    S.op(DVE, lambda: nc.vector.scalar_tensor_tensor(
                        out=dst_ap[:, tc * 512:(tc + 1) * 512], in0=qs.ap, scalar=gvec.ap, in1=sd.ap,
                        op0=ALU.mult, op1=ALU.mult), reads=[qs, gvec, sd], writes=[dst_t[tc]])

                return unit_a, unit_b

            def proj_standalone(h_):
                ua0, ub0 = make_units(h_)
                for t in range(len(tiles_)):
                    ua0(t, t % 4)
                    if t >= 2:
                        ub0(t - 2, 4 + t % 2)
                for t in range(max(0, len(tiles_) - 2), len(tiles_)):
                    ub0(t, 4 + t % 2)

            proj_standalone(0)

            for h in range(H):
                hp = h // 2
                if h % 2 == 0 and hp + 1 < H // 2:
                    load_qk(hp + 1)
                par = h % 2
                steps = [(qc, j) for qc in range(TG) for j in range(4 * qc + 4)]
                qt, kt, qtt, ktt = QT[par], KT[par], QTt[par], KTt[par]

                def s_stage(si):
                    qc, j = steps[si]
                    il0 = max(0, j - 4 * qc)
                    ncols = 512 - il0 * 128
                    q0 = qc * 512 + il0 * 128
                    b0 = 2 * (si % 2)
                    et = ET[si % 3]
                    diag = j >= 4 * qc
                    for s_ in range(2):
                        S.op(PE, lambda: nc.tensor.matmul(PS[:, b0 + s_, 0:ncols],
                                                          lhsT=kt[s_ * 64:(s_ + 1) * 64, j * 128:(j + 1) * 128],
                                                          rhs=qt[s_ * 64:(s_ + 1) * 64, q0:q0 + ncols],
                                                          start=True, stop=not diag),
                             reads=[ktt[j // 4], qtt[qc]], writes=[BK[b0 + s_]], signal=not diag)
                    if diag:
                        for s_ in range(2):
                            S.op(PE, lambda: nc.tensor.matmul(PS[:, b0 + s_, 0:128], lhsT=IDENT.ap, rhs=TRI.ap,
                                                              start=False, stop=True),
                                 reads=[IDENT, TRI], writes=[BK[b0 + s_]])
                    S.op(ACT, lambda: nc.scalar.activation(out=et.ap[:, :, 0:ncols], in_=PS[:, b0:b0 + 2, 0:ncols],
                                                           func=AF.Exp, scale=0.125),
                         reads=[BK[b0], BK[b0 + 1]], writes=[et])

                def av_stage(si):
                    qc, j = steps[si]
                    il0 = max(0, j - 4 * qc)
                    et = ET[si % 3]
                    mms = []
                    for il in range(il0, 4):
                        for s_ in range(2):
                            bk_, off = acc_slots[il * 2 + s_]
                            mms.append((bk_, off, il, s_))
                    for n_, (bk_, off, il, s_) in enumerate(mms):
                        c0 = (il - il0) * 128
                        S.op(PE, lambda: nc.tensor.matmul(PS[:, bk_, off:off + 129],
                                                          lhsT=et.ap[:, s_, c0:c0 + 128],
                                                          rhs=VA_ap[:, j, h, 0:129],
                                                          start=(j == 0 and off == 0), stop=(j == 4 * qc + il),
                                                          skip_group_check=True),
                             reads=[et, VA[j][h // 4]], writes=[BK[bk_]], signal=(n_ == len(mms) - 1))
                    if j == 4 * qc + 3:
                        evac_chunk(qc)

                def evac_chunk(qc, h=h):
                    k_ = nchunk[0] % 2
                    nchunk[0] += 1
                    as_, rc, ss, on = AS[k_], RC[k_], SS4[k_], ON[k_]
                    as4 = as_.ap.rearrange("p (s i) c -> p s i c", i=4)
                    S.op(DVE, lambda: nc.vector.tensor_copy(
                        out=as4[:, :, 0:3, :],
                        in_=PS[:, 4:6, 0:387].rearrange("p s (a c) -> p s a c", c=129)),
                        reads=[BK[4], BK[5]], writes=[as_])
                    S.op(DVE, lambda: nc.vector.tensor_copy(
                        out=as4[:, :, 3, :], in_=PS[:, 6, 0:258].rearrange("p (a c) -> p a c", c=129)),
                        reads=[BK[6]], writes=[as_])
                    if boundary_units:
                        boundary_units.pop(0)()
                    flush_deferred()
                    S.op(DVE, lambda: nc.vector.reciprocal(out=rc.ap, in_=as_.ap[:, :, 128]), reads=[as_],
                         writes=[rc])
                    S.op(DVE, lambda: nc.vector.tensor_scalar(out=rc.ap[:, 4:8], in0=rc.ap[:, 4:8],
                                                              scalar1=NEGLAM.ap, scalar2=None, op0=ALU.mult),
                         reads=[rc, NEGLAM], writes=[rc])
                    E_, e_ = (POOL, nc.gpsimd) if POOL_OFFLOAD else (DVE, nc.vector)
                    S.op(E_, lambda: e_.tensor_tensor(
                        out=as_.ap[:, :, 0:128], in0=as_.ap[:, :, 0:128],
                        in1=rc.ap.unsqueeze(2).to_broadcast([128, 8, 128]), op=ALU.mult),
                        reads=[as_, rc], writes=[as_])
                    S.op(E_, lambda: e_.tensor_tensor(out=as_.ap[:, 0:4, 0:128], in0=as_.ap[:, 0:4, 0:128],
                                                      in1=as_.ap[:, 4:8, 0:128], op=ALU.add),
                         reads=[as_], writes=[as_])
                    S.op(DVE, lambda: nc.vector.tensor_tensor(out=as_.ap[:, 4:8, 0:128], in0=as_.ap[:, 0:4, 0:128],
                                                              in1=as_.ap[:, 0:4, 0:128], op=ALU.mult),
                         reads=[as_], writes=[as_])
                    S.op(DVE, lambda: nc.vector.reduce_sum(out=ss.ap, in_=as_.ap[:, 4:8, 0:128], axis=AX.X),
                         reads=[as_], writes=[ss])
                    S.op(DVE, lambda: nc.vector.tensor_scalar(out=ss.ap, in0=ss.ap, scalar1=1.0 / 128, scalar2=EPS,
                                                              op0=ALU.mult, op1=ALU.add), reads=[ss], writes=[ss])
                    S.op(POOL, lambda: nc.gpsimd.tensor_tensor(out=ss.ap, in0=ss.ap, in1=MHALF.ap[:, 0:4],
                                                               op=ALU.pow), reads=[ss, MHALF], writes=[ss])
                    S.op(POOL, lambda: nc.gpsimd.tensor_tensor(
                        out=on.ap, in0=as_.ap[:, 0:4, 0:128], in1=ss.ap.unsqueeze(2).to_broadcast([128, 4, 128]),
                        op=ALU.mult), reads=[as_, ss], writes=[on])

                    def fin():
                        for il in range(4):
                            S.op(PE, lambda: nc.tensor.transpose(bank_bf(7)[:, il * 128:(il + 1) * 128],
                                                                 on.ap[:, il, :], IDENT.ap),
                                 reads=[on, IDENT], writes=[BK[7]], signal=(il == 3))
                        S.op(DVE, lambda: nc.vector.tensor_scalar(out=oT_ap[:, h, qc * 512:(qc + 1) * 512],
                                                                  in0=bank_bf(7)[:, 0:512], scalar1=SUBG.ap,
                                                                  scalar2=None, op0=ALU.mult),
                             reads=[BK[7], SUBG], writes=[oT[h][qc]])
                    deferred.append(fin)

                sched_ = {}
                boundary_units = []
                if h + 1 < H and INTERLEAVE_PROJ:
                    ua, ub = make_units(h + 1)
                    sp_ = max(2, len(steps) // len(tiles_))
                    nb_ = min(TG, len(tiles_))
                    for t in range(nb_):
                        boundary_units.append(lambda t=t, ua=ua: ua(t, 7))
                    rest = list(range(nb_, len(tiles_)))
                    slots_ = []
                    for qc_ in range(1, TG):
                        base = sum(4 * q_ + 4 for q_ in range(qc_))
                        ln_ = 4 * qc_ + 4
                        npos = 1 if qc_ < TG - 1 else max(1, len(rest) - (TG - 2))
                        for p_ in range(npos):
                            slots_.append(base + (p_ + 1) * ln_ // (npos + 1) - 1)
                    for t, st_ in zip(rest, slots_):
                        sched_.setdefault(st_, []).append(lambda t=t, ua=ua: ua(t, 7, 0))
                        sched_.setdefault(st_ + 1, []).append(lambda t=t, ua=ua: ua(t, 7, 1))
                    for t in rest[len(slots_):]:
                        sched_.setdefault(10 ** 6 + t, []).append(lambda t=t, ua=ua: ua(t, 7))
                flush_deferred()
                s_stage(0)
                if len(steps) > 1:
                    s_stage(1)
                for si in range(len(steps)):
                    if si + 2 < len(steps):
                        s_stage(si + 2)
                    av_stage(si)
                    for u_ in sched_.pop(si, []):
                        u_()
                while boundary_units:
                    boundary_units.pop(0)()
                for k_ in sorted(sched_):
                    for u_ in sched_[k_]:
                        u_()
                if h + 1 < H and not INTERLEAVE_PROJ:
                    proj_standalone(h + 1)
                if h + 1 < H and INTERLEAVE_PROJ:
                    for t in range(len(tiles_)):
                        ub(t, 4 + t % 2)

            flush_deferred()
            if b == 0:
                dump("VA", VA_ap, allva)
                dump("QT", QT[1], QTt[1])
                dump("KT", KT[1], KTt[1])
                dump("oT", oT_ap, [t_ for h_ in range(H) for t_ in oT[h_]])
            R34.reset()
            M.reset()
            ya_ap = R34.view(0, [128, KC, SQ], BF16)
            yaT = [[R34.tile(ya_ap[:, f, g * 512:(g + 1) * 512]) for g in range(TG)] for f in range(KC)]
            mT_ap = R34.view(16 * SQ, [128, KC, SQ], BF16)
            mT = [[R34.tile(mT_ap[:, f, g * 512:(g + 1) * 512]) for g in range(TG)] for f in range(KC)]
            WC = [[M.tile(M.view((i * 3 + j) * 4096, [128, KC, 256], BF16)) for j in range(3)] for i in range(2)]
            for i in range(2):
                for j in range(3):
                    WC[i][j].dsem = wc_ds[i * 3 + j]
            o = 24576
            U_ap = M.view(o, [128, 2 + SQ], F32)
            Ut = [M.tile(U_ap[:, 2 + g * 512: 2 + (g + 1) * 512]) for g in range(TG)]
            Upad = M.tile(U_ap[:, 0:2])
            o += (2 + SQ) * 4
            o = (o + 31) // 32 * 32
            CCS = [M.tile(M.view(o + i * 2048, [128, 512], F32)) for i in range(2)]
            o += 4096
            YB = [M.tile(M.view(o + i * 2048, [128, 512], F32)) for i in range(2)]
            o += 4096
            assert o <= M.nbytes, o
            S.op(POOL, lambda: nc.gpsimd.memset(U_ap[:, 0:2], 0.0), writes=[Upad])

            def load_conv(fp):
                sl = fp % 2
                for j, base in enumerate((4096, 5120, 3072)):
                    load_w(WC[sl][j], WC[sl][j].ap, win_v[:, :, base + fp * 256: base + (fp + 1) * 256])

            load_conv(0)
            cidx = 0
            for f in range(KC):
                fp = f // 2
                if f % 2 == 0 and fp + 1 < KC // 2:
                    load_conv(fp + 1)
                Wcc, Wcx, Wcb = WC[fp % 2]
                for tc in range(TG):
                    bks = [(cidx % 2) * 3 + i for i in range(3)]
                    for W_, b_ in zip((Wcc, Wcx, Wcb), bks):
                        mmgroup(PS[:, b_, :], [(W_.ap[:, kc, (f % 2) * 128:(f % 2 + 1) * 128],
                                                hT[kc][0][:, tc * 512:(tc + 1) * 512]) for kc in range(KC)],
                                BK[b_], [hT[kc][1][tc] for kc in range(KC)] + [W_])
                    ccs = CCS[cidx % 2]
                    yb = YB[cidx % 2]
                    S.op(ACT, lambda: nc.scalar.copy(out=ccs.ap, in_=PS[:, bks[0], :]), reads=[BK[bks[0]]],
                         writes=[ccs])
                    ut = Ut[tc]
                    S.op(DVE, lambda: nc.vector.tensor_tensor(out=ut.ap, in0=PS[:, bks[1], :], in1=ccs.ap,
                                                              op=ALU.mult), reads=[BK[bks[1]], ccs], writes=[ut])
                    prev = [Ut[tc - 1]] if tc > 0 else [Upad]
                    c0 = 2 + tc * 512
                    S.op(POOL, lambda: nc.gpsimd.tensor_scalar(out=yb.ap, in0=U_ap[:, c0 - 2:c0 - 2 + 512],
                                                               scalar1=CW.ap[:, 0, f:f + 1], scalar2=0.0,
                                                               op0=ALU.mult, op1=ALU.add),
                         reads=[ut, CW] + prev, writes=[yb])
                    S.op(DVE, lambda: nc.vector.scalar_tensor_tensor(out=yb.ap, in0=U_ap[:, c0 - 1:c0 - 1 + 512],
                                                                      scalar=CW.ap[:, 1, f:f + 1], in1=yb.ap,
                                                                      op0=ALU.mult, op1=ALU.add),
                         reads=[ut, CW, yb] + prev, writes=[yb])
                    S.op(DVE, lambda: nc.vector.scalar_tensor_tensor(out=yb.ap, in0=U_ap[:, c0:c0 + 512],
                                                                      scalar=CW.ap[:, 2, f:f + 1], in1=yb.ap,
                                                                      op0=ALU.mult, op1=ALU.add),
                         reads=[ut, CW, yb], writes=[yb])
                    S.op(DVE, lambda: nc.vector.tensor_tensor(out=ya_ap[:, f, tc * 512:(tc + 1) * 512],
                                                              in0=PS[:, bks[2], :], in1=yb.ap, op=ALU.mult),
                         reads=[BK[bks[2]], yb], writes=[yaT[f][tc]])
                    cidx += 1

            M.reset()
            WD4 = [[M.tile(M.view((i * 4 + j) * 4096, [128, KC, 256], BF16)) for j in range(4)] for i in range(2)]
            for i in range(2):
                for j in range(4):
                    WD4[i][j].dsem = wd_ds[i * 4 + j]
            o = 32768
            SG = [[M.tile(M.view(o + (i * 2 + j) * 2048, [128, 512], F32)) for j in range(2)] for i in range(2)]
            o += 8192
            M1 = [[M.tile(M.view(o + (i * 2 + j) * 2048, [128, 512], F32)) for j in range(2)] for i in range(2)]
            o += 8192
            assert o <= M.nbytes

            def load_d(fp):
                sl = fp % 2
                sl_c = slice(fp * 256, (fp + 1) * 256)
                load_w(WD4[sl][0], WD4[sl][0].ap, wa_v[:, :, sl_c])
                load_w(WD4[sl][1], WD4[sl][1].ap, wb_v[:, :, sl_c])
                load_w(WD4[sl][2], WD4[sl][2].ap, win_v[:, :, 6144 + fp * 256: 6144 + (fp + 1) * 256])
                load_w(WD4[sl][3], WD4[sl][3].ap, win_v[:, :, 7168 + fp * 256: 7168 + (fp + 1) * 256])

            load_d(0)
            didx = 0
            for f in range(KC):
                fp = f // 2
                if f % 2 == 0 and fp + 1 < KC // 2:
                    load_d(fp + 1)
                Wa_, Wb_, Wga_, Wgb_ = WD4[fp % 2]
                cs = slice((f % 2) * 128, (f % 2 + 1) * 128)
                for tc in range(TG):
                    ts = slice(tc * 512, (tc + 1) * 512)
                    bks = [(didx % 2) * 4 + i for i in range(4)]
                    mmgroup(PS[:, bks[0], :], [(Wga_.ap[:, kc, cs], hT[kc][0][:, ts]) for kc in range(KC)],
                            BK[bks[0]], [hT[kc][1][tc] for kc in range(KC)] + [Wga_])
                    mmgroup(PS[:, bks[1], :], [(Wgb_.ap[:, kc, cs], hT[kc][0][:, ts]) for kc in range(KC)],
                            BK[bks[1]], [hT[kc][1][tc] for kc in range(KC)] + [Wgb_])
                    mmgroup(PS[:, bks[2], :], [(Wa_.ap[:, kc, cs], ya_ap[:, kc, ts]) for kc in range(KC)],
                            BK[bks[2]], [yaT[kc][tc] for kc in range(KC)] + [Wa_])
                    mmgroup(PS[:, bks[3], :], [(Wb_.ap[:, kc, cs], oT_ap[:, kc, ts]) for kc in range(KC)],
                            BK[bks[3]], [oT[kc][tc] for kc in range(KC)] + [Wb_])
                    sga, sgb = SG[didx % 2]
                    m1, m2 = M1[didx % 2]
                    S.op(ACT, lambda: nc.scalar.activation(out=sga.ap, in_=PS[:, bks[0], :], func=AF.Sigmoid),
                         reads=[BK[bks[0]]], writes=[sga])
                    S.op(ACT, lambda: nc.scalar.activation(out=sgb.ap, in_=PS[:, bks[1], :], func=AF.Sigmoid),
                         reads=[BK[bks[1]]], writes=[sgb])
                    S.op(DVE, lambda: nc.vector.tensor_tensor(out=m1.ap, in0=PS[:, bks[2], :], in1=sga.ap,
                                                              op=ALU.mult), reads=[BK[bks[2]], sga], writes=[m1])
                    S.op(DVE, lambda: nc.vector.tensor_tensor(out=m2.ap, in0=PS[:, bks[3], :], in1=sgb.ap,
                                                              op=ALU.mult), reads=[BK[bks[3]], sgb], writes=[m2])
                    S.op(POOL, lambda: nc.gpsimd.tensor_tensor(out=mT_ap[:, f, ts], in0=m1.ap, in1=m2.ap,
                                                               op=ALU.add), reads=[m1, m2], writes=[mT[f][tc]])
                    didx += 1

            if b == 0:
                dump("ya", ya_ap, [t_ for f_ in range(KC) for t_ in yaT[f_]])
                dump("mT", mT_ap, [t_ for f_ in range(KC) for t_ in mT[f_]])
            R1.reset()
            R2.reset()
            M.reset()
            x1a = R1.view(0, [128, TB // 2, D], F32)
            x1b = R2.view(0, [128, TB // 2, D], F32)

            def x1_ap(tb):
                return (x1a if tb < TB // 2 else x1b)[:, tb % (TB // 2), :]

            X1 = [(R1 if tb < TB // 2 else R2).tile(x1_ap(tb)) for tb in range(TB)]
            ya_dead = S.retire([yaT[f][g] for f in range(KC) for g in range(TG)])
            WO = T(R34.view(0, [128, KC, D], BF16), ya_dead)
            WO.dsem = wo_ds
            R34.tiles.append(WO)
            for hf in range(2):
                load_w(WO, WO.ap[:, :, hf * 512:(hf + 1) * 512], wo_v[:, :, hf * 512:(hf + 1) * 512])
            XIN = [M.tile(M.view(i * 4096, [128, 1024], F32)) for i in range(2)]
            for i in range(2):
                XIN[i].dsem = xin_ds[i]
            TMP = [M.tile(M.view(8192 + i * 2048, [128, 512], F32)) for i in range(2)]
            G1BC = M.tile(M.view(12288, [128, D], F32))
            G1BC.dsem = g_ds[0]
            S.dma(SP, G1BC.ap, gbc_d[b, 0], G1BC.dsem, reads=[GBC_D[b][0]], writes=[G1BC])
            eidx = 0
            for tb in range(TB):
                xin = XIN[tb % 2]
                S.dma(SP, xin.ap, x_d[b, tb * 128:(tb + 1) * 128, :], xin.dsem, writes=[xin])
                for hf in range(2):
                    b_ = nextbank()
                    hs = slice(hf * 512, (hf + 1) * 512)
                    mmgroup(PS[:, b_, :], [(mT_ap[:, kc, tb * 128:(tb + 1) * 128], WO.ap[:, kc, hs])
                                           for kc in range(KC)], BK[b_],
                            [mT[kc][tb // 4] for kc in range(KC)] + [WO])
                    tmp = TMP[eidx % 2]
                    S.op(DVE, lambda: nc.vector.tensor_tensor(out=tmp.ap, in0=PS[:, b_, :], in1=G1BC.ap[:, hs],
                                                              op=ALU.mult), reads=[BK[b_], G1BC], writes=[tmp])
                    S.op(POOL, lambda: nc.gpsimd.tensor_tensor(out=x1_ap(tb)[:, hs], in0=tmp.ap, in1=xin.ap[:, hs],
                                                               op=ALU.add), reads=[tmp, xin], writes=[X1[tb]])
                    eidx += 1

            if b == 0:
                dump("x1a", x1a, X1[:TB // 2], F32)
                dump("x1b", x1b, X1[TB // 2:], F32)
            R34.reset()
            M.reset()
            TMP = [M.tile(M.view(i * 1024, [128, 256], F32)) for i in range(2)]
            SGB_ = [M.tile(M.view(2048 + i * 2048, [128, 512], F32)) for i in range(2)]
            G2BC = M.tile(M.view(6144, [128, D], F32))
            G2BC.dsem = g_ds[1]
            S.dma(SP, G2BC.ap, gbc_d[b, 1], G2BC.dsem, reads=[GBC_D[b][1]], writes=[G2BC])
            XN = [M.tile(M.view(10240 + i * 2048, [128, 1024], BF16)) for i in range(4)]
            WG = [M.tile(M.view(18432 + i * 8192, [128, KC, 2, 256], BF16)) for i in range(2)]
            for i in range(2):
                WG[i].dsem = wg_ds[i]
            o = 18432 + 16384
            WDN = [M.tile(M.view(o + i * 11264, [128, FC, 256], BF16)) for i in range(2)]
            for i in range(2):
                WDN[i].dsem = wdn_ds[i]
            o += 22528
            assert o <= M.nbytes, o
            NCH = SQ // HS
            h2_ap = R34.view(0, [128, KC, HS], BF16)
            aT_ap = R34.view(16 * HS, [128, FC, HS], BF16)
            NG = HS // 512

            def load_gu(gp):
                t_ = WG[gp % 2]
                load_w(t_, t_.ap[:, :, 0, :], wgu_v[:, :, gp * 256:(gp + 1) * 256])
                load_w(t_, t_.ap[:, :, 1, :], wgu_v[:, :, DFF + gp * 256: DFF + (gp + 1) * 256])

            def load_dn(cp):
                t_ = WDN[cp % 2]
                load_w(t_, t_.ap[:, 0:11, :], wdn_v[:, 0:11, cp * 256:(cp + 1) * 256])
                load_w(t_, t_.ap[:, 11:22, :], wdn_v[:, 11:22, cp * 256:(cp + 1) * 256])

            def mk_tiles(aps, deps):
                ts_ = [T(ap, deps) for ap in aps]
                R34.tiles.extend(ts_)
                return ts_

            def mk_h2T(deps):
                return [(h2_ap[:, kc, :], mk_tiles([h2_ap[:, kc, g * 512:(g + 1) * 512] for g in range(NG)], deps))
                        for kc in range(KC)]

            def mk_aT(deps):
                return [mk_tiles([aT_ap[:, fc, g * 512:(g + 1) * 512] for g in range(NG)], deps)
                        for fc in range(FC)]

            def norm_part(ch, g):
                tb0 = ch * (HS // 128)
                return [norm_tile(X1[tb0 + g * 4 + i], x1_ap(tb0 + g * 4 + i), SSQ[i], XN[i]) for i in range(4)]

            def trans_part(h2T, g, xns):
                transpose_group(xns, h2T, G2s, S2s, b, g * 512, (tgc[0] % 2) * 4)
                tgc[0] += 1

            def gu_phase(h2T, aT, last):
                gidx = 0
                for gp in range(FC // 2):
                    if gp + 1 < FC // 2:
                        load_gu(gp + 1)
                    else:
                        load_dn(0)
                    wg = WG[gp % 2]
                    for f2 in range(2):
                        fc = gp * 2 + f2
                        for g in range(NG):
                            ts = slice(g * 512, (g + 1) * 512)
                            bks = [(gidx % 4) * 2, (gidx % 4) * 2 + 1]
                            for u_ in range(2):
                                mmgroup(PS[:, bks[u_], :],
                                        [(wg.ap[:, kc, u_, f2 * 128:(f2 + 1) * 128], h2T[kc][0][:, ts])
                                         for kc in range(KC)], BK[bks[u_]],
                                        [h2T[kc][1][g] for kc in range(KC)] + [wg])
                            sg = SGB_[gidx % 2]
                            S.op(ACT, lambda: nc.scalar.activation(out=sg.ap, in_=PS[:, bks[0], :], func=AF.Silu),
                                 reads=[BK[bks[0]]], writes=[sg])
                            S.op(DVE, lambda: nc.vector.tensor_tensor(out=aT_ap[:, fc, ts], in0=PS[:, bks[1], :],
                                                                      in1=sg.ap, op=ALU.mult),
                                 reads=[BK[bks[1]], sg], writes=[aT[fc][g]])
                            gidx += 1

            def down_phase(ch, aT, hooks):
                nonlocal_e = ecnt
                tb0 = ch * (HS // 128)
                for cp in range(4):
                    if cp + 1 < 4:
                        load_dn(cp + 1)
                    wd = WDN[cp % 2]
                    cs = slice(cp * 256, (cp + 1) * 256)
                    for tbl in range(HS // 128):
                        tb = tb0 + tbl
                        b_ = nextbank()
                        mmgroup(PS[:, b_, 0:256], [(aT_ap[:, fc, tbl * 128:(tbl + 1) * 128], wd.ap[:, fc, :])
                                                   for fc in range(FC)], BK[b_],
                                [aT[fc][tbl // 4] for fc in range(FC)] + [wd])
                        tmp = TMP[nonlocal_e[0] % 2]
                        S.op(DVE, lambda: nc.vector.tensor_tensor(out=tmp.ap[:, 0:256], in0=PS[:, b_, 0:256],
                                                                  in1=G2BC.ap[:, cs], op=ALU.mult),
                             reads=[BK[b_], G2BC], writes=[tmp])
                        S.op(POOL, lambda: nc.gpsimd.tensor_tensor(out=x1_ap(tb)[:, cs], in0=tmp.ap[:, 0:256],
                                                                   in1=x1_ap(tb)[:, cs], op=ALU.add),
                             reads=[tmp, X1[tb]], writes=[X1[tb]])
                        nonlocal_e[0] += 1
                        if cp == 3:
                            S.dma(SP, out_d[b, tb * 128:(tb + 1) * 128, :], x1_ap(tb), out_ds[tb], reads=[X1[tb]])
                    for hk in hooks.get(cp, []):
                        hk()

            ecnt = [0]
            h2T_c = mk_h2T(R34.fence)
            aT_c = mk_aT(R34.fence)
            load_gu(0)
            for g in range(NG):
                trans_part(h2T_c, g, norm_part(0, g))
            for ch in range(NCH):
                gu_phase(h2T_c, aT_c, ch == NCH - 1)
                hooks = {}
                if ch + 1 < NCH:
                    h2T_n = mk_h2T(S.retire([t_ for kc in range(KC) for t_ in h2T_c[kc][1]]))
                    pend = {}
                    pend[0] = norm_part(ch + 1, 0)

                    def mk_hook(g, h2T_n=h2T_n, pend=pend, ch=ch):
                        def hk():
                            trans_part(h2T_n, g, pend[g])
                            if g + 1 < NG:
                                pend[g + 1] = norm_part(ch + 1, g + 1)
                            else:
                                load_gu(0)
                        return hk
                    for g in range(NG):
                        hooks.setdefault(min(g, 3), []).append(mk_hook(g))
                if ch == NCH - 1 and b + 1 < NB and TG <= 4:
                    dead1 = S.retire(X1[:TB // 2])
                    hT_ap_n = R1.view(0, [128, KC, SQ], BF16)
                    hT_n = []
                    for kc in range(KC):
                        ts_ = [T(hT_ap_n[:, kc, g * 512:(g + 1) * 512], dead1) for g in range(TG)]
                        R1.tiles.extend(ts_)
                        hT_n.append((hT_ap_n[:, kc, :], ts_))
                    wg_dead = S.retire(WG)
                    XINp = [T(M.view(18432 + i * 4096, [128, 1024], F32), wg_dead) for i in range(4)]
                    M.tiles.extend(XINp)
                    for i in range(4):
                        XINp[i].dsem = xin_ds[4 + i]
                    pendn = {}

                    def load_x_n(g):
                        for i in range(4):
                            tb = g * 4 + i
                            S.dma(SP, XINp[i].ap, x_d[b + 1, tb * 128:(tb + 1) * 128, :], XINp[i].dsem,
                                  writes=[XINp[i]])

                    def norm_n(g):
                        pendn[g] = [norm_tile(XINp[i], XINp[i].ap, SSQ[i], XN[i]) for i in range(4)]

                    def mk_hook_n(g, hT_n=hT_n):
                        def hk():
                            transpose_group(pendn[g], hT_n, G1s, S1s, b + 1, g * 512, (tgc[0] % 2) * 4)
                            tgc[0] += 1
                            if g + 1 < TG:
                                load_x_n(g + 1)
                                norm_n(g + 1)
                        return hk
                    load_x_n(0)
                    norm_n(0)
                    for g in range(TG):
                        hooks.setdefault(g + (4 - TG), []).append(mk_hook_n(g))
                    next_hT[0] = (hT_ap_n, hT_n)
                down_phase(ch, aT_c, hooks)
                if ch + 1 < NCH:
                    aT_c = mk_aT(S.retire([t_ for fc in range(FC) for t_ in aT_c[fc]]))
                    h2T_c = h2T_n
            R34.reset()

        for ds in out_ds + [dbg_ds]:
            if ds.val:
                SP.eng.wait_ge(ds.sem, ds.val)
    return nc


_CACHE = {}


def _layout_inputs(inp, NB):
    f = lambda a: np.ascontiguousarray(np.asarray(a, dtype=np.float32))
    x = f(inp["x"])
    c = f(inp["c"])
    shared = {
        "w_ada": f(inp["w_ada"][0]),
        "b_adaT": f(inp["b_ada"][0].reshape(48, 128).T),
        "b_ada_row": f(inp["b_ada"][0].reshape(1, 6 * D)),
        "n1gT": f(inp["norm1_g"][0].reshape(KC, 128).T),
        "n2gT": f(inp["norm2_g"][0].reshape(KC, 128).T),
        "w_in": f(inp["w_in"][0]),
        "cwT": f(np.asarray(inp["conv_w"][0]).reshape(3, KC, 128).transpose(2, 0, 1)),
        "gq": f(np.tile(np.asarray(inp["q_norm_g"][0]), 2).reshape(128, 1)),
        "gk": f(np.tile(np.asarray(inp["k_norm_g"][0]), 2).reshape(128, 1)),
        "lamv": f(np.concatenate([np.asarray(inp[k_][0]) for k_ in
                                  ("lambda_q1", "lambda_k1", "lambda_q2", "lambda_k2")]).reshape(1, 256)),
        "subg": f(np.asarray(inp["subln_g"][0]).reshape(128, 1)),
        "w_a_out": f(inp["w_a_out"][0]),
        "w_b_out": f(inp["w_b_out"][0]),
        "w_o": f(inp["w_o"][0]),
        "w_gu": f(inp["w_gu"][0]),
        "w_down": f(inp["w_down"][0]),
    }
    n_cores = x.shape[0] // NB
    maps = []
    for i in range(n_cores):
        m = dict(shared)
        m["x"] = np.ascontiguousarray(x[i * NB:(i + 1) * NB])
        cs = c[i * NB:(i + 1) * NB]
        m["cT"] = np.ascontiguousarray(cs.reshape(NB, KC, 128).transpose(2, 1, 0))
        maps.append(m)
    return maps


def kernel(**inputs):
    x = np.asarray(inputs["x"])
    B, SQ, _ = x.shape
    NB = B // N_CORES
    key = (NB, SQ)
    if key not in _CACHE:
        _CACHE[key] = build(NB, SQ)
    nc = _CACHE[key]
    maps = _layout_inputs(inputs, NB)
    res = run_bass_kernel_spmd(nc, maps, core_ids=list(range(N_CORES)))
    out = np.concatenate([np.asarray(r["out"]) for r in res.results], axis=0)
    return out.astype(np.float32, copy=False)
```
